# Optimizing a Trainium2 kernel written in Bass

```python
import jax, jax.numpy as jnp
from jax import lax
import numpy as np

D_MODEL = 1024
BATCH = 8
SEQ = 2048
DEPTH = 2

N_META = 16
FOX_HEADS = 8
FOX_HEAD_DIM = 64
FOX_WIDTH = FOX_HEADS * FOX_HEAD_DIM
Q_BLOCK = 128
CONV_CH = D_MODEL - FOX_WIDTH
CONV_WIDTH = 31
IN_COLS = 3 * FOX_WIDTH + FOX_HEADS + 2 * CONV_CH
POOL_WINDOWS = (2, 4, 8, 16)
N_POOL_GROUPS = len(POOL_WINDOWS)
POOL_GROUP = D_MODEL // N_POOL_GROUPS
D_FF = 2816
FFN_CONV_WIDTH = 3
RMS_EPS = 1e-6
LN_EPS = 1e-5
N_EVEN = (DEPTH + 1) // 2
N_ODD = DEPTH // 2

kernel_name = "fox_conformer_pool_hybrid_block"


def rms_norm(x, g):
    xf = x.astype(jnp.float32)
    y = xf * lax.rsqrt(jnp.mean(xf * xf, axis=-1, keepdims=True) + RMS_EPS)
    return (y * g.astype(jnp.float32)).astype(x.dtype)


def layer_norm(x, g, b):
    xf = x.astype(jnp.float32)
    mu = jnp.mean(xf, axis=-1, keepdims=True)
    var = jnp.mean(jnp.square(xf - mu), axis=-1, keepdims=True)
    y = (xf - mu) * lax.rsqrt(var + LN_EPS)
    return (y * g.astype(jnp.float32) + b.astype(jnp.float32)).astype(x.dtype)


def causal_depthwise_conv(x, w, b):
    K, C = w.shape
    y = lax.conv_general_dilated(
        x, w[:, None, :].astype(x.dtype), window_strides=(1,), padding=[(K - 1, 0)],
        dimension_numbers=('NWC', 'WIO', 'NWC'), feature_group_count=C)
    return y + b.astype(x.dtype)


def forgetting_attention(q, k, v, log_f):
    B, L, H, Dh = q.shape
    n_blk = (L - N_META) // Q_BLOCK
    qf, kf, vf = (a.astype(jnp.float32) for a in (q, k, v))
    c = jnp.transpose(jnp.cumsum(log_f, axis=1), (0, 2, 1))
    scale = Dh ** -0.5
    pos = jnp.arange(L)

    def attend(q_blk, c_q, q_pos):
        s = jnp.einsum('bqhd,bkhd->bhqk', q_blk, kf) * scale
        s = s + c_q[..., :, None] - c[..., None, :]
        s = jnp.where(pos[None, :] <= q_pos[:, None], s, -jnp.inf)
        p = jax.nn.softmax(s, axis=-1)
        return jnp.einsum('bhqk,bkhd->bqhd', p, vf)

    out_meta = attend(qf[:, :N_META], c[:, :, :N_META], pos[:N_META])
    q_r = jnp.transpose(qf[:, N_META:].reshape(B, n_blk, Q_BLOCK, H, Dh), (1, 0, 2, 3, 4))
    c_r = jnp.transpose(c[:, :, N_META:].reshape(B, H, n_blk, Q_BLOCK), (2, 0, 1, 3))
    p_r = pos[N_META:].reshape(n_blk, Q_BLOCK)
    out_r = lax.map(lambda a: attend(*a), (q_r, c_r, p_r))
    out_r = jnp.transpose(out_r, (1, 0, 2, 3, 4)).reshape(B, L - N_META, H, Dh)
    out = jnp.concatenate([out_meta, out_r], axis=1)
    return out.reshape(B, L, H * Dh)


def fox_conformer_mixer(h, w_in, b_f, conv_w, conv_b, ln_g, ln_b, w_out):
    B, L, _ = h.shape
    proj = h @ w_in.astype(h.dtype)
    q, k, v, f_logit, glu = jnp.split(
        proj, [FOX_WIDTH, 2 * FOX_WIDTH, 3 * FOX_WIDTH, 3 * FOX_WIDTH + FOX_HEADS], axis=-1)
    log_f = jax.nn.log_sigmoid(f_logit.astype(jnp.float32) + b_f.astype(jnp.float32))
    shp = (B, L, FOX_HEADS, FOX_HEAD_DIM)
    attn = forgetting_attention(q.reshape(shp), k.reshape(shp), v.reshape(shp), log_f).astype(h.dtype)
    a, g = jnp.split(glu, 2, axis=-1)
    u = a * jax.nn.sigmoid(g)
    u = causal_depthwise_conv(u, conv_w, conv_b)
    u = jax.nn.silu(layer_norm(u, ln_g, ln_b))
    return jnp.concatenate([attn, u], axis=-1) @ w_out.astype(h.dtype)


def multiscale_pool_mixer(h, pool_w, pool_b, pool_scale):
    B, L, D = h.shape
    hf = h.astype(jnp.float32).reshape(B, L, N_POOL_GROUPS, POOL_GROUP)
    cs = jnp.cumsum(hf, axis=1)
    n_seen = jnp.arange(1, L + 1)
    outs = []
    for gi, w in enumerate(POOL_WINDOWS):
        csg = cs[:, :, gi]
        lag = jnp.pad(csg, ((0, 0), (w, 0), (0, 0)))[:, :L]
        cnt = jnp.minimum(n_seen, w).astype(jnp.float32)[None, :, None]
        outs.append((csg - lag) / cnt - hf[:, :, gi])
    d = jnp.stack(outs, axis=2)
    y = jnp.einsum('blgc,gcd->blgd', d, pool_w.astype(jnp.float32)) + pool_b.astype(jnp.float32)
    return (y.reshape(B, L, D) * pool_scale.astype(jnp.float32)).astype(h.dtype)


def conv_glu_ffn(h, w_up, conv_w, conv_b, w_down):
    u = h @ w_up.astype(h.dtype)
    u = causal_depthwise_conv(u, conv_w, conv_b)
    gate, val = jnp.split(u, 2, axis=-1)
    return (jax.nn.silu(gate) * val) @ w_down.astype(h.dtype)


def setup_inputs(seed: int = 0) -> dict:
    key = jax.random.key(seed)
    ks = jax.random.split(key, 24)
    nrm = lambda k, shp, s: jax.random.normal(k, shp, jnp.float32) * s
    D = D_MODEL
    return {
        "x": nrm(ks[0], (BATCH, SEQ, D), 1.0),
        "meta_tokens": nrm(ks[1], (N_META, D), 1.0),
        "mix_norm_even": 1.0 + nrm(ks[2], (N_EVEN, D), 0.02),
        "w_in": nrm(ks[3], (N_EVEN, D, IN_COLS), D ** -0.5),
        "b_f": jax.random.uniform(ks[4], (N_EVEN, FOX_HEADS), jnp.float32, 2.0, 5.0),
        "conv_w": nrm(ks[5], (N_EVEN, CONV_WIDTH, CONV_CH), CONV_WIDTH ** -0.5),
        "conv_b": nrm(ks[6], (N_EVEN, CONV_CH), 0.02),
        "ln_g": 1.0 + nrm(ks[7], (N_EVEN, CONV_CH), 0.02),
        "ln_b": nrm(ks[8], (N_EVEN, CONV_CH), 0.02),
        "w_out": nrm(ks[9], (N_EVEN, FOX_WIDTH + CONV_CH, D), (FOX_WIDTH + CONV_CH) ** -0.5),
        "mix_norm_odd": 1.0 + nrm(ks[10], (N_ODD, D), 0.02),
        "pool_w": nrm(ks[11], (N_ODD, N_POOL_GROUPS, POOL_GROUP, POOL_GROUP), POOL_GROUP ** -0.5),
        "pool_b": nrm(ks[12], (N_ODD, N_POOL_GROUPS, POOL_GROUP), 0.02),
        "pool_scale": 0.5 + nrm(ks[13], (N_ODD, D), 0.1),
        "ffn_norm": 1.0 + nrm(ks[14], (DEPTH, D), 0.02),
        "w_up": nrm(ks[15], (DEPTH, D, 2 * D_FF), D ** -0.5),
        "ffn_conv_w": nrm(ks[16], (DEPTH, FFN_CONV_WIDTH, 2 * D_FF), FFN_CONV_WIDTH ** -0.5),
        "ffn_conv_b": nrm(ks[17], (DEPTH, 2 * D_FF), 0.02),
        "w_down": nrm(ks[18], (DEPTH, D_FF, D), D_FF ** -0.5),
        "final_norm": 1.0 + nrm(ks[19], (D,), 0.02),
    }


def reference(x, meta_tokens, mix_norm_even, w_in, b_f, conv_w, conv_b, ln_g, ln_b, w_out,
              mix_norm_odd, pool_w, pool_b, pool_scale,
              ffn_norm, w_up, ffn_conv_w, ffn_conv_b, w_down, final_norm):
    B = x.shape[0]
    meta = jnp.broadcast_to(meta_tokens.astype(x.dtype)[None], (B, N_META, x.shape[-1]))
    h = jnp.concatenate([meta, x], axis=1)
    for i in range(DEPTH):
        j = i // 2
        if i % 2 == 0:
            h = h + fox_conformer_mixer(rms_norm(h, mix_norm_even[j]), w_in[j], b_f[j], conv_w[j],
                                        conv_b[j], ln_g[j], ln_b[j], w_out[j])
        else:
            h = h + multiscale_pool_mixer(rms_norm(h, mix_norm_odd[j]), pool_w[j], pool_b[j],
                                          pool_scale[j])
        h = h + conv_glu_ffn(rms_norm(h, ffn_norm[i]), w_up[i], ffn_conv_w[i], ffn_conv_b[i],
                             w_down[i])
    h = rms_norm(h, final_norm)
    return h[:, N_META:]
```

```python
import os
import numpy as np
import concourse.bass as bass
import concourse.mybir as mybir
from concourse.bass_utils import run_bass_kernel_spmd
from contextlib import ExitStack

F32 = mybir.dt.float32
BF16 = mybir.dt.bfloat16
ALU = mybir.AluOpType
AF = mybir.ActivationFunctionType

D = 1024
L = 2064
NMETA = 16
SEQ = 2048
DFF = 2816
NCH = 8
TT = [(0, 16), (16, 512), (528, 512), (1040, 512), (1552, 512)]
KB = [(0, 16)] + [(16 + 128 * j, 128) for j in range(16)]
PT = [(0, 413), (413, 413), (826, 413), (1239, 413), (1652, 412)]
FT = [(0, 416, 0), (414, 414, 2), (826, 414, 2), (1238, 414, 2), (1650, 414, 2)]
GROUPS = [(0, 6, 0), (6, 5, 1), (11, 6, 0), (17, 5, 1)]

O_MNE, O_MNO, O_FFN0, O_FFN1, O_FIN = 0, 8, 16, 24, 32
O_CONVW, O_CONVB, O_LNG, O_LNB = 40, 164, 168, 172
O_POOLB, O_POOLS = 176, 184
O_FCW, O_FCB = 192, 456
O_BF = 544
O_INVC = 545
NVH = 609
O_NBF = 609
O_PBS = 610
NV = 618


class Acc:
    __slots__ = ("ap", "space", "ranges")

    def __init__(self, ap, space, ranges):
        self.ap = ap
        self.space = space
        self.ranges = ranges


class View:
    def __init__(self, space, ap, off, esize, dims):
        self.space, self.ap, self.off, self.esize, self.dims = space, ap, off, esize, tuple(dims)

    def __call__(self, *idx, p=None):
        ps = slice(None) if p is None else slice(p[0], p[1])
        es, off = self.esize, self.off
        if len(self.dims) == 1:
            (a, b), = idx if idx else ((0, self.dims[0]),)
            return Acc(self.ap[ps, a:b], self.space, [(off + a * es, off + b * es)])
        R, C = self.dims
        r = idx[0]
        a, b = idx[1] if len(idx) > 1 else (0, C)
        if isinstance(r, int):
            return Acc(self.ap[ps, r, a:b], self.space, [(off + (r * C + a) * es, off + (r * C + b) * es)])
        r0, r1 = r
        return Acc(self.ap[ps, r0:r1, a:b], self.space,
                   [(off + (q * C + a) * es, off + (q * C + b) * es) for q in range(r0, r1)])

    def whole(self):
        n = int(np.prod(self.dims))
        return Acc(self.ap, self.space, [(self.off, self.off + n * self.esize)])


class Op:
    __slots__ = ("eng", "fn", "deps", "is_dma", "token", "signal", "uid", "prev_same_sem")

    def __init__(self, eng, fn, is_dma, uid):
        self.eng, self.fn, self.is_dma, self.uid = eng, fn, is_dma, uid
        self.deps = set()
        self.token = None
        self.signal = False
        self.prev_same_sem = None


class Tracker:
    def __init__(self):
        self.spaces = {}

    def _segs(self, space):
        if space not in self.spaces:
            self.spaces[space] = [[0, 1 << 40, None, {}]]
        return self.spaces[space]

    def access(self, op, space, lo, hi, write):
        segs = self._segs(space)
        out = []
        deps = op.deps
        for seg in segs:
            s_lo, s_hi, w, rd = seg
            if s_hi <= lo or s_lo >= hi:
                out.append(seg)
                continue
            if s_lo < lo:
                out.append([s_lo, lo, w, dict(rd)])
                s_lo = lo
            tail = None
            if s_hi > hi:
                tail = [hi, s_hi, w, dict(rd)]
                s_hi = hi
            if w is not None and w is not op:
                deps.add(w)
            if write:
                for r in rd.values():
                    if r is not op:
                        deps.add(r)
                out.append([s_lo, s_hi, op, {}])
            else:
                rd = dict(rd)
                key = ("d", op.uid) if op.is_dma else op.eng
                rd[key] = op
                out.append([s_lo, s_hi, w, rd])
            if tail is not None:
                out.append(tail)
        merged = []
        for seg in out:
            if merged and merged[-1][2] is seg[2] and merged[-1][3] == seg[3] and merged[-1][1] == seg[0]:
                merged[-1][1] = seg[1]
            else:
                merged.append(seg)
        self.spaces[space] = merged


ENGS = ("pe", "act", "dve", "pool", "sp")
NDMASEM = 8


class Prog:
    def __init__(self):
        self.q = {e: [] for e in ENGS}
        self.tr = Tracker()
        self.uid = 0

    def add(self, eng, fn, reads=(), writes=(), dma=False):
        self.uid += 1
        op = Op(eng, fn, dma, self.uid)
        for a in reads:
            for lo, hi in a.ranges:
                self.tr.access(op, a.space, lo, hi, False)
        for a in writes:
            for lo, hi in a.ranges:
                self.tr.access(op, a.space, lo, hi, True)
        self.q[eng].append(op)
        return op

    def pe(self, fn, reads=(), writes=()):
        return self.add("pe", fn, reads, writes)

    def act(self, fn, reads=(), writes=()):
        return self.add("act", fn, reads, writes)

    def dve(self, fn, reads=(), writes=()):
        return self.add("dve", fn, reads, writes)

    def pool(self, fn, reads=(), writes=()):
        return self.add("pool", fn, reads, writes)

    def dma(self, queue, out, in_, reads=(), writes=()):
        return self.add(queue, lambda e: e.dma_start(out=out, in_=in_), reads, writes, dma=True)

    def fence(self, queue, ops):
        self.uid += 1
        op = Op(queue, None, False, self.uid)
        op.deps = set(ops)
        self.q[queue].append(op)
        return op

    def finalize(self, sems, dma_sems):
        for e in ENGS:
            for op in self.q[e]:
                for d in op.deps:
                    d.signal = True
        for e in ENGS:
            cnt = 0
            ndma = 0
            last_on_sem = {}
            for op in self.q[e]:
                if op.is_dma:
                    si = ndma % NDMASEM
                    sem = dma_sems[e][si]
                    op.prev_same_sem = last_on_sem.get(si)
                    op.token = (sem, 16 * (ndma // NDMASEM + 1))
                    last_on_sem[si] = op
                    op.signal = True
                    ndma += 1
                elif op.signal:
                    cnt += 1
                    op.token = (sems[e], cnt)

    def emit(self, eng_name, e):
        known = {}
        for op in self.q[eng_name]:
            waits = {}
            deps = list(op.deps)
            if op.prev_same_sem is not None:
                deps.append(op.prev_same_sem)
            for d in deps:
                if eng_name == "pe" and d.eng == "pe" and not d.is_dma:
                    continue
                sem, val = d.token
                k = id(sem)
                if k not in waits or waits[k][1] < val:
                    waits[k] = (sem, val)
            for k, (sem, val) in waits.items():
                if known.get(k, 0) < val:
                    e.wait_ge(sem, val)
                    known[k] = val
            if op.fn is None:
                continue
            ins = op.fn(e)
            if op.signal:
                sem, val = op.token
                ins.then_inc(sem, 16 if op.is_dma else 1)


class RR:
    def __init__(self, items):
        self.items, self.i = items, 0

    def next(self):
        v = self.items[self.i % len(self.items)]
        self.i += 1
        return v


def build_program(stage=None):
    nc = bass.Bass("TRN2", target_bir_lowering=False)
    P = Prog()
    es = ExitStack()

    def dram(name, shape, kind="ExternalInput"):
        return nc.dram_tensor(name, shape, F32, kind=kind).ap()

    xT = dram("xT", [D, SEQ])
    metaT = dram("metaT", [D, NMETA])
    vecs_d = dram("vecs", [128, NVH])
    w_in = dram("w_in", [D, 2568])
    w_out = dram("w_out", [D, D])
    pool_w = dram("pool_w", [4, 256, 256])
    w_up = dram("w_up", [2, D, 2 * DFF])
    w_down = dram("w_down", [2, DFF, D])
    outT = dram("outT", [D, SEQ], kind="ExternalOutput")
    dbg = dram("dbg", [128, NCH * L], kind="ExternalOutput") if stage is not None else None
    c3d = nc.dram_tensor("c3d", [8, 3, L], BF16, kind="Internal").ap()

    def sb(name, shape, dt):
        return es.enter_context(nc.sbuf_tensor(name, shape, dt))

    Ht = sb("H", [128, NCH, L], F32)
    HNt = sb("HN", [128, NCH * L], BF16)
    BIGt = sb("BIG", [128, 35840], BF16)
    VECt = sb("VEC", [128, NV], F32)
    IDt = sb("ident", [128, 128], BF16)
    TRIt = sb("tri", [128, 128], BF16)
    ONEt = sb("ones", [128, 128], BF16)
    WFt = sb("wf", [128, 8, 8], BF16)
    WCHt = [sb(f"wch{i}", [128, 8, 128], BF16) for i in range(6)]
    S2Kt = [sb(f"s2k{i}", [128, 512], F32) for i in range(6)]
    S1Kt = [sb(f"s1k{i}", [128, 512], BF16) for i in range(12)]
    PSt = [es.enter_context(nc.psum_tensor(f"ps{i}", [128, 512], F32)) for i in range(8)]

    H = View("H", Ht[:], 0, 4, (NCH, L))
    HN = View("HN", HNt[:].rearrange("p (c t) -> p c t", c=NCH), 0, 2, (NCH, L))
    Y = View("HN", HNt[:].bitcast(F32).rearrange("p (c t) -> p c t", c=4), 0, 4, (4, L))
    VEC = View("VEC", VECt[:], 0, 4, (NV,))
    IDN = View("ident", IDt[:], 0, 2, (128,))
    TRI = View("tri", TRIt[:], 0, 2, (128,))
    ONE = View("ones", ONEt[:], 0, 2, (128,))
    WF = View("wf", WFt[:], 0, 2, (8, 8))
    WCH = RR([View(f"wch{i}", t[:], 0, 2, (8, 128)) for i, t in enumerate(WCHt)])
    S2K = RR([View(f"s2k{i}", t[:], 0, 4, (512,)) for i, t in enumerate(S2Kt)])
    S1K = RR([View(f"s1k{i}", t[:], 0, 2, (512,)) for i, t in enumerate(S1Kt)])
    PSG = RR([View(f"ps{i}", t[:], 0, 4, (512,)) for i, t in enumerate(PSt[:5])])
    PSA = RR([View(f"ps{i}", t[:], 0, 4, (512,)) for i, t in enumerate(PSt[5:], start=5)])
    S2K4 = RR(S2K.items[:4])
    PS8 = RR([View(f"ps{i}", t[:], 0, 4, (512,)) for i, t in enumerate(PSt)])

    def big(off, dt, dims):
        esz = 4 if dt == F32 else 2
        n = int(np.prod(dims))
        ap = BIGt[:, off // 2: off // 2 + n * esz // 2]
        if dt == F32:
            ap = ap.bitcast(F32)
        if len(dims) == 2:
            ap = ap.rearrange("p (a b) -> p a b", a=dims[0])
        elif len(dims) == 3:
            ap = ap.rearrange("p (a b c) -> p a b c", a=dims[0], b=dims[1])
        return ap

    def vcol(j, p=None):
        return VEC((j, j + 1), p=p)

    dbg_ops = []

    def dump_and_finish():
        o = P.dma("sp", dbg, Ht[:].rearrange("p c t -> p (c t)"), reads=[H.whole()])
        P.fence("sp", [o])

    P.dma("sp", VECt[:, 0:NVH], vecs_d, writes=[VEC((0, NVH))])
    P.dma("sp", Ht[:, :, 0:NMETA], metaT.rearrange("(c p) t -> p c t", p=128),
          writes=[H((0, NCH), (0, NMETA))])
    for (s, W) in PT:
        a = max(s, NMETA)
        P.dma("sp", Ht[:, :, a:s + W], xT[:, a - NMETA:s + W - NMETA].rearrange("(c p) t -> p c t", p=128),
              writes=[H((0, NCH), (a, s + W))])
    P.pool(lambda e: e.memset(ONEt[:], 1.0), writes=[ONE.whole()])
    P.pool(lambda e: e.memset(IDt[:], 1.0), writes=[IDN.whole()])
    P.pool(lambda e: e.affine_select(out=IDt[:], in_=IDt[:], pattern=[[-1, 128]], compare_op=ALU.is_equal,
                                     fill=0.0, base=0, channel_multiplier=1),
           reads=[IDN.whole()], writes=[IDN.whole()])
    P.pool(lambda e: e.memset(TRIt[:], 1.0), writes=[TRI.whole()])
    P.pool(lambda e: e.affine_select(out=TRIt[:], in_=TRIt[:], pattern=[[1, 128]], compare_op=ALU.is_ge,
                                     fill=0.0, base=0, channel_multiplier=-1),
           reads=[TRI.whole()], writes=[TRI.whole()])
    P.dve(lambda e: e.tensor_scalar(out=VECt[:, O_NBF:O_NBF + 1], in0=VECt[:, O_BF:O_BF + 1], scalar1=-1.0,
                                    scalar2=None, op0=ALU.mult),
          reads=[vcol(O_BF)], writes=[vcol(O_NBF)])
    P.dve(lambda e: e.tensor_tensor(out=VECt[:, O_PBS:O_PBS + 8], in0=VECt[:, O_POOLB:O_POOLB + 8],
                                    in1=VECt[:, O_POOLS:O_POOLS + 8], op=ALU.mult),
          reads=[VEC((O_POOLB, O_POOLS + 8))], writes=[VEC((O_PBS, O_PBS + 8))])

    def matmul(out, lhsT, rhs, start, stop):
        P.pe(lambda e: e.matmul(out.ap, lhsT.ap, rhs.ap, start=start, stop=stop),
             reads=[lhsT, rhs], writes=[out])

    def load_w(dst_acc, src_ap):
        P.dma("pool", dst_acc.ap, src_ap, writes=[dst_acc])

    def norm_stats(s, W, out_acc, src=H, nrows=NCH, scale=1.0 / D, eps=1e-6, sq_pool=None, defer=False, sq_split=False):
        pb = PSG.next()
        for c in range(nrows):
            sq = (sq_pool or S1K).next()
            a_in, a_sq = src(c, (s, s + W)), sq((0, W))
            if sq_split and c % 2 == 1:
                P.dve(lambda e, a_in=a_in, a_sq=a_sq: e.tensor_tensor(out=a_sq.ap, in0=a_in.ap, in1=a_in.ap, op=ALU.mult),
                      reads=[a_in], writes=[a_sq])
            else:
                P.act(lambda e, a_in=a_in, a_sq=a_sq: e.activation(out=a_sq.ap, in_=a_in.ap, func=AF.Square),
                      reads=[a_in], writes=[a_sq])
            matmul(pb((0, W)), ONE.whole(), a_sq, c == 0, c == nrows - 1)
        a_pb = pb((0, W))
        if defer:
            return lambda: norm_fin(a_pb, out_acc, scale, eps)
        norm_fin(a_pb, out_acc, scale, eps)

    def norm_fin(a_pb, out_acc, scale, eps):
        P.act(lambda e: e.activation(out=out_acc.ap, in_=a_pb.ap, func=AF.Ln, scale=scale, bias=eps),
              reads=[a_pb], writes=[out_acc])
        P.act(lambda e: e.activation(out=out_acc.ap, in_=out_acc.ap, func=AF.Exp, scale=-0.5),
              reads=[out_acc], writes=[out_acc])

    def norm_tile(goff, dst, s, W, sq_pool=None, r_pool=None):
        r = (r_pool or S2K).next()((0, W))
        norm_stats(s, W, r, sq_pool=sq_pool)
        for c in range(NCH):
            a_h, a_g, a_o = H(c, (s, s + W)), vcol(goff + c), dst(c, s, W)
            P.dve(lambda e, a_h=a_h, a_g=a_g, a_o=a_o, r=r: e.scalar_tensor_tensor(
                out=a_o.ap, in0=a_h.ap, scalar=a_g.ap, in1=r.ap, op0=ALU.mult, op1=ALU.mult),
                reads=[a_h, a_g, r], writes=[a_o])

    def rmsnorm(goff, dst, after_tile=None):
        for (s, W) in PT:
            norm_tile(goff, dst, s, W)
            if after_tile is not None:
                after_tile(s, W)

    def hn_dst(c, s, W):
        return HN(c, (s, s + W))

    def resid_add(d, s, W, pb_acc):
        a_h = H(d, (s, s + W))
        P.dve(lambda e: e.tensor_tensor(out=a_h.ap, in0=pb_acc.ap, in1=a_h.ap, op=ALU.add),
              reads=[pb_acc, a_h], writes=[a_h])

    def mixer0():
        VAS = [big(i * 8704, BF16, (17, 2, 128)) for i in range(2)]
        QKS = [[View("BIG", big(17408 + b_ * 16512 + i * 4128, BF16, (L,)), 17408 + b_ * 16512 + i * 4128, 2, (L,))
                for i in range(4)] for b_ in range(2)]
        C3 = View("BIG", big(33920, BF16, (3, L)), 33920, 2, (3, L))
        T1 = View("BIG", big(55168, F32, (L,)), 55168, 4, (L,))
        T2 = View("BIG", big(63424, F32, (L,)), 63424, 4, (L,))
        CATLO = View("BIG", big(55168, BF16, (4, L)), 55168, 2, (4, L))
        U = View("BIG", big(0, BF16, (4, L + 30)), 0, 2, (4, L + 30))
        CATHI = View("BIG", big(36864, BF16, (4, L)), 36864, 2, (4, L))
        DG = [View("BIG", big(16768 + i * 7936, BF16, (31, 128)), 16768 + i * 7936, 2, (31, 128)) for i in range(2)]

        def va_acc(vb, j, hh, c0, c1, kw):
            lo = vb * 8704 + ((j * 2 + hh) * 128 + c0) * 2
            hi = vb * 8704 + ((j * 2 + hh) * 128 + c1) * 2
            return Acc(VAS[vb][0:kw, j, hh, c0:c1], "BIG", [(lo, hi)])

        def proj_pieces(pr):
            bsel = pr % 2
            qA, qB, kA, kB = QKS[bsel]
            w3 = []

            def loads():
                wq, wk, wv = WCH.next(), WCH.next(), WCH.next()
                load_w(wq.whole(), w_in[:, pr * 128:(pr + 1) * 128].rearrange("(kc p) n -> p kc n", p=128))
                load_w(wk.whole(), w_in[:, 512 + pr * 128:512 + (pr + 1) * 128].rearrange("(kc p) n -> p kc n", p=128))
                load_w(wv.whole(), w_in[:, 1024 + pr * 128:1024 + (pr + 1) * 128].rearrange("(kc p) n -> p kc n", p=128))
                w3.extend([wq, wk, wv])
            pieces = [loads]
            for which in range(2):
                tA, tB = (qA, qB) if which == 0 else (kA, kB)
                scl, fillv, row0 = (0.125, 1.0, 64) if which == 0 else (1.0, -1.0, 67)

                def ptile(s, W, which=which, tA=tA, tB=tB, scl=scl):
                    wt = w3[which]
                    pb = PSG.next()
                    for kc in range(8):
                        matmul(pb((0, W)), wt(kc), HN(kc, (s, s + W)), kc == 0, kc == 7)
                    for hh, tX in enumerate((tA, tB)):
                        a_pb, a_o = pb((0, W), p=(hh * 64, hh * 64 + 64)), tX((s, s + W), p=(0, 64))
                        P.dve(lambda e, a_pb=a_pb, a_o=a_o: e.tensor_scalar(
                            out=a_o.ap, in0=a_pb.ap, scalar1=scl, scalar2=None, op0=ALU.mult),
                            reads=[a_pb], writes=[a_o])
                for (s, W) in PT:
                    pieces.append(lambda s=s, W=W, ptile=ptile: ptile(s, W))

                def post(tA=tA, tB=tB, fillv=fillv, row0=row0):
                    for hh, tX in enumerate((tA, tB)):
                        a_r = tX((0, L), p=(row0, row0 + 3))
                        P.dma("sp", a_r.ap, c3d[2 * pr + hh], reads=[c3d_acc], writes=[a_r])
                pieces.append(post)

            def vblock(j):
                ks, kw = KB[j]
                wv = w3[2]
                pb = PSG.next()
                for kc in range(8):
                    matmul(pb((0, 128), p=(0, kw)), HN(kc, (ks, ks + kw)), wv(kc), kc == 0, kc == 7)
                a_pb = pb((0, 128), p=(0, kw))
                a_pb3 = Acc(a_pb.ap.rearrange("p (h d) -> p h d", h=2), a_pb.space, a_pb.ranges)
                a_v = Acc(VAS[bsel][0:kw, j, :, 0:64], "BIG", [(bsel * 8704 + j * 512, bsel * 8704 + j * 512 + 512)])
                P.dve(lambda e: e.tensor_copy(out=a_v.ap, in_=a_pb3.ap), reads=[a_pb3], writes=[a_v])
            for j in range(len(KB)):
                pieces.append(lambda j=j: vblock(j))
            return pieces

        def core(pr, nxt_pieces):
            bsel = pr % 2
            qA, qB, kA, kB = QKS[bsel]
            step = 0
            pending_norm = []
            for hh in range(2):
                qa, ka = (qA, kA) if hh == 0 else (qB, kB)
                for qi, (qs, qw) in enumerate(TT):
                    if qi == 0:
                        blocks = [(0, 0, 16, True)]
                    else:
                        nfull = 4 * (qi - 1)
                        blocks = [(j, 0, 512, False) for j in range(0, nfull + 1)]
                        blocks += [(nfull + 1 + m, 128 * m, 512 - 128 * m, True) for m in range(4)]
                    po = PSA.next()
                    nb = len(blocks)
                    LOOK = 3
                    ptiles = [None] * nb
                    for idx in range(nb + LOOK):
                        if idx < nb:
                            j, off, n, diag = blocks[idx]
                            ks, kw = KB[j]
                            ps_ = PSG.next()
                            a_s = ps_((0, n), p=(0, kw))
                            matmul(a_s, ka((ks, ks + kw), p=(0, 70)), qa((qs + off, qs + off + n), p=(0, 70)), True, True)
                            pt = S1K.next()
                            a_p = pt((0, n), p=(0, kw))
                            P.act(lambda e, a_s=a_s, a_p=a_p: e.activation(out=a_p.ap, in_=a_s.ap, func=AF.Exp),
                                  reads=[a_s], writes=[a_p])
                            if diag:
                                mw = min(128, n)
                                a_pm, a_tr = pt((0, mw), p=(0, kw)), TRI((0, mw), p=(0, kw))
                                P.dve(lambda e, a_pm=a_pm, a_tr=a_tr: e.tensor_tensor(
                                    out=a_pm.ap, in0=a_pm.ap, in1=a_tr.ap, op=ALU.mult),
                                    reads=[a_pm, a_tr], writes=[a_pm])
                            ptiles[idx] = a_p
                            if pending_norm and idx == min(1, nb - 1):
                                pending_norm.pop(0)()
                        k2 = idx - LOOK
                        if k2 >= 0:
                            j, off, n, diag = blocks[k2]
                            ks, kw = KB[j]
                            matmul(po((off, off + n)), va_acc(bsel, j, hh, 0, 128, kw), ptiles[k2], k2 == 0, k2 == nb - 1)
                            step += 1
                            if nxt_pieces and step % 3 == 0:
                                nxt_pieces.pop(0)()
                    def normalize(po=po, qs=qs, qw=qw, hh=hh):
                        rb = S2K.next()
                        a_den, a_rb = po((0, qw), p=(64, 128)), rb((0, qw), p=(0, 64))
                        if qw > 64:
                            P.dve(lambda e: e.reciprocal(out=a_rb.ap, in_=a_den.ap), reads=[a_den], writes=[a_rb])
                        else:
                            P.act(lambda e: e.activation(out=a_rb.ap, in_=a_den.ap, func=AF.Ln), reads=[a_den], writes=[a_rb])
                            P.act(lambda e: e.activation(out=a_rb.ap, in_=a_rb.ap, func=AF.Exp, scale=-1.0),
                                  reads=[a_rb], writes=[a_rb])
                        a_num = po((0, qw), p=(0, 64))
                        a_cat = CATLO(pr, (qs, qs + qw), p=(hh * 64, hh * 64 + 64))
                        P.dve(lambda e: e.tensor_tensor(out=a_cat.ap, in0=a_num.ap, in1=a_rb.ap, op=ALU.mult),
                              reads=[a_num, a_rb], writes=[a_cat])
                    pending_norm.append(normalize)
            while pending_norm:
                pending_norm.pop(0)()
            while nxt_pieces:
                nxt_pieces.pop(0)()

        c3d_acc = Acc(c3d, "C3D", [(0, 8 * 3 * L * 2)])
        pieces0 = proj_pieces(0)
        load_w(WF.whole(), w_in[:, 1536:1544].rearrange("(kc p) n -> p kc n", p=128))
        pieces0[0]()
        for vb in range(2):
            va_all = Acc(VAS[vb][:, :, :, 64:128], "BIG", [(vb * 8704, vb * 8704 + 8704)])
            P.pool(lambda e, va_all=va_all: e.memset(va_all.ap, 1.0), reads=[], writes=[va_all])
        for b_ in range(2):
            for i, fv in enumerate((1.0, 1.0, -1.0, -1.0)):
                a_aug = QKS[b_][i]((0, L), p=(64, 70))
                P.pool(lambda e, a_aug=a_aug, fv=fv: e.memset(a_aug.ap, fv), writes=[a_aug])

        q_t, q_post, k_t, k_post, v_b = pieces0[1:6], pieces0[6], pieces0[7:12], pieces0[12], pieces0[13:]
        norm_tile(O_MNE, hn_dst, *PT[0])
        for i, (s, W) in enumerate(PT):
            if i + 1 < len(PT):
                norm_tile(O_MNE, hn_dst, *PT[i + 1])
            pb = PSG.next()
            for kc in range(8):
                matmul(pb((0, W), p=(0, 8)), WF(kc), HN(kc, (s, s + W)), kc == 0, kc == 7)
            a_pb, a_t, a_b = pb((0, W), p=(0, 8)), T1((s, s + W), p=(0, 8)), vcol(O_NBF, p=(0, 8))
            P.act(lambda e, a_pb=a_pb, a_t=a_t, a_b=a_b: e.activation(out=a_t.ap, in_=a_pb.ap, func=AF.Exp,
                                                                     scale=-1.0, bias=a_b.ap),
                  reads=[a_pb, a_b], writes=[a_t])
            t1i, t2i = T1((s, s + W), p=(0, 8)), T2((s, s + W), p=(0, 8))
            P.act(lambda e, t1i=t1i: e.activation(out=t1i.ap, in_=t1i.ap, func=AF.Ln, bias=1.0),
                  reads=[t1i], writes=[t1i])
            if i == 0:
                P.dve(lambda e, t1i=t1i, t2i=t2i: e.tensor_tensor_scan(
                    out=t2i.ap, data0=t1i.ap, data1=t1i.ap, initial=0.0, op0=ALU.add, op1=ALU.max),
                    reads=[t1i], writes=[t2i])
            else:
                a_init = T2((s - 1, s), p=(0, 8))
                P.dve(lambda e, t1i=t1i, t2i=t2i, a_init=a_init: e.tensor_tensor_scan(
                    out=t2i.ap, data0=t1i.ap, data1=t1i.ap, initial=a_init.ap, op0=ALU.add, op1=ALU.max),
                    reads=[t1i, a_init], writes=[t2i])
            c0i, c1i, c2i = (C3(j, (s, s + W), p=(0, 8)) for j in range(3))
            P.pool(lambda e, c0i=c0i, t2i=t2i: e.tensor_copy(out=c0i.ap, in_=t2i.ap), reads=[t2i], writes=[c0i])
            P.pool(lambda e, t1i=t1i, t2i=t2i, c0i=c0i: e.tensor_tensor(out=t1i.ap, in0=t2i.ap, in1=c0i.ap, op=ALU.subtract),
                   reads=[t2i, c0i], writes=[t1i])
            P.pool(lambda e, c1i=c1i, t1i=t1i: e.tensor_copy(out=c1i.ap, in_=t1i.ap), reads=[t1i], writes=[c1i])
            P.pool(lambda e, t1i=t1i, c1i=c1i: e.tensor_tensor(out=t1i.ap, in0=t1i.ap, in1=c1i.ap, op=ALU.subtract),
                   reads=[t1i, c1i], writes=[t1i])
            P.pool(lambda e, c2i=c2i, t1i=t1i: e.tensor_copy(out=c2i.ap, in_=t1i.ap), reads=[t1i], writes=[c2i])
            q_t[i]()
            k_t[i]()
        for piece in v_b:
            piece()
        c3_all = C3((0, 3), (0, L), p=(0, 8))
        P.dma("sp", c3d, c3_all.ap, reads=[c3_all], writes=[c3d_acc])
        q_post()
        k_post()
        for pr in range(4):
            nxt = proj_pieces(pr + 1) if pr + 1 < 4 else []
            core(pr, nxt)

        a_pad = U((0, 4), (0, 30))
        P.dve(lambda e: e.memset(a_pad.ap, 0.0), writes=[a_pad])
        for m in range(4):
            wa, wg = WCH.next(), WCH.next()
            load_w(wa.whole(), w_in[:, 1544 + m * 128:1544 + (m + 1) * 128].rearrange("(kc p) n -> p kc n", p=128))
            load_w(wg.whole(), w_in[:, 2056 + m * 128:2056 + (m + 1) * 128].rearrange("(kc p) n -> p kc n", p=128))
            for (s, W) in PT:
                pa, pg = PSG.next(), PSG.next()
                for kc in range(8):
                    matmul(pa((0, W)), wa(kc), HN(kc, (s, s + W)), kc == 0, kc == 7)
                for kc in range(8):
                    matmul(pg((0, W)), wg(kc), HN(kc, (s, s + W)), kc == 0, kc == 7)
                sg = S2K.next()
                a_pg, a_sg, a_pa, a_u = pg((0, W)), sg((0, W)), pa((0, W)), U(m, (30 + s, 30 + s + W))
                P.act(lambda e, a_pg=a_pg, a_sg=a_sg: e.activation(out=a_sg.ap, in_=a_pg.ap, func=AF.Sigmoid),
                      reads=[a_pg], writes=[a_sg])
                P.dve(lambda e, a_pa=a_pa, a_sg=a_sg, a_u=a_u: e.tensor_tensor(
                    out=a_u.ap, in0=a_pa.ap, in1=a_sg.ap, op=ALU.mult), reads=[a_pa, a_sg], writes=[a_u])
        def conv_tile(m, s, W):
            dg = DG[(m + 1) % 2]
            npe = 25 if m < 3 else 31
            pb = PSG.next()
            for k in range(npe):
                matmul(pb((0, W)), dg(k), U(m, (s + k, s + k + W)), k == 0, k == npe - 1)
            a_pb, a_y, a_b = pb((0, W)), Y(m, (s, s + W)), vcol(O_CONVB + m)
            P.act(lambda e: e.activation(out=a_y.ap, in_=a_pb.ap, func=AF.Identity, bias=a_b.ap),
                  reads=[a_pb, a_b], writes=[a_y])
            for k in range(npe, 31):
                a_u, a_w = U(m, (s + k, s + k + W)), vcol(O_CONVW + m * 31 + k)
                P.dve(lambda e, a_u=a_u, a_w=a_w: e.scalar_tensor_tensor(
                    out=a_y.ap, in0=a_u.ap, scalar=a_w.ap, in1=a_y.ap, op0=ALU.mult, op1=ALU.add),
                    reads=[a_u, a_w, a_y], writes=[a_y])

        for m in range(4):
            dg = DG[(m + 1) % 2]
            for k in range(31):
                a_d, a_w = dg(k), vcol(O_CONVW + m * 31 + k)
                P.pool(lambda e, a_d=a_d, a_w=a_w: e.tensor_scalar(
                    out=a_d.ap, in0=IDt[:], scalar1=a_w.ap, scalar2=0.0, op0=ALU.mult, op1=ALU.add),
                    reads=[IDN.whole(), a_w], writes=[a_d])
            if m < 3:
                for (s, W) in PT:
                    conv_tile(m, s, W)
        ffn_group(0, 0, True, False, load_part="lo")
        wos = [WCH.next() for _ in range(6)] + [
            View(f"s2k{i}", S2Kt[i][:].bitcast(BF16).rearrange("p (a b) -> p a b", a=8), 0, 2, (8, 128)) for i in (4, 5)]
        for d in range(NCH):
            load_w(wos[d].whole(), w_out[:, d * 128:(d + 1) * 128].rearrange("(kc p) n -> p kc n", p=128))
        lnt = S2K.items[:4]

        def ln_tile(s, W):
            pm, pq = PSG.next(), PSG.next()
            for m in range(4):
                ysq, ybf = S1K.next(), S1K.next()
                a_y, a_sq, a_bf = Y(m, (s, s + W)), ysq((0, W)), ybf((0, W))
                P.act(lambda e, a_y=a_y, a_sq=a_sq: e.activation(out=a_sq.ap, in_=a_y.ap, func=AF.Square),
                      reads=[a_y], writes=[a_sq])
                P.dve(lambda e, a_y=a_y, a_bf=a_bf: e.tensor_copy(out=a_bf.ap, in_=a_y.ap), reads=[a_y], writes=[a_bf])
                matmul(pm((0, W)), ONE.whole(), a_bf, m == 0, m == 3)
                matmul(pq((0, W)), ONE.whole(), a_sq, m == 0, m == 3)
            mean, var = lnt[0]((0, W)), lnt[1]((0, W))
            a_pm, a_pq = pm((0, W)), pq((0, W))
            P.dve(lambda e, a_pm=a_pm, mean=mean: e.tensor_scalar(
                out=mean.ap, in0=a_pm.ap, scalar1=1.0 / 512, scalar2=None, op0=ALU.mult), reads=[a_pm], writes=[mean])
            P.dve(lambda e, mean=mean, var=var: e.tensor_tensor(out=var.ap, in0=mean.ap, in1=mean.ap, op=ALU.mult),
                  reads=[mean], writes=[var])
            P.dve(lambda e, a_pq=a_pq, var=var: e.scalar_tensor_tensor(
                out=var.ap, in0=a_pq.ap, scalar=1.0 / 512, in1=var.ap, op0=ALU.mult, op1=ALU.subtract),
                reads=[a_pq, var], writes=[var])
            P.act(lambda e, var=var: e.activation(out=var.ap, in_=var.ap, func=AF.Ln, bias=1e-5),
                  reads=[var], writes=[var])
            P.act(lambda e, var=var: e.activation(out=var.ap, in_=var.ap, func=AF.Exp, scale=-0.5),
                  reads=[var], writes=[var])
            for m in range(4):
                t1_ = lnt[2 + m % 2]((0, W))
                a_y = Y(m, (s, s + W))
                P.dve(lambda e, a_y=a_y, mean=mean, t1_=t1_: e.tensor_tensor(
                    out=t1_.ap, in0=a_y.ap, in1=mean.ap, op=ALU.subtract), reads=[a_y, mean], writes=[t1_])
                P.dve(lambda e, var=var, t1_=t1_: e.tensor_tensor(
                    out=t1_.ap, in0=t1_.ap, in1=var.ap, op=ALU.mult), reads=[t1_, var], writes=[t1_])
                a_g, a_b, a_z = vcol(O_LNG + m), vcol(O_LNB + m), CATHI(m, (s, s + W))
                P.act(lambda e, t1_=t1_, a_g=a_g, a_b=a_b, a_z=a_z: e.activation(
                    out=a_z.ap, in_=t1_.ap, func=AF.Silu, scale=a_g.ap, bias=a_b.ap),
                    reads=[t1_, a_g, a_b], writes=[a_z])

        conv_tile(3, *NT[0])
        conv_tile(3, *NT[1])
        ln_tile(*NT[0])
        pending = []
        for ti, (s, W) in enumerate(NT):
            if ti + 2 < len(NT):
                conv_tile(3, *NT[ti + 2])
                if ti + 2 == len(NT) - 1:
                    ffn_group(0, 0, True, False, load_part="hi")
            if ti + 1 < len(NT):
                ln_tile(*NT[ti + 1])
            for d in range(NCH):
                pb = PSG.next()
                for kc in range(8):
                    rhs = CATLO(kc, (s, s + W)) if kc < 4 else CATHI(kc - 4, (s, s + W))
                    matmul(pb((0, W)), wos[d](kc), rhs, kc == 0, kc == 7)
                resid_add(d, s, W, pb((0, W)))
            pending.append((s, W))
            if ti + 1 >= len(NT) - 1:
                for (s2, W2) in pending:
                    norm_tile(O_FFN0, hn_dst, s2, W2, r_pool=S2K4)
                pending = []
        for (s2, W2) in pending:
            norm_tile(O_FFN0, hn_dst, s2, W2, r_pool=S2K4)

    GB_OFF = [0, 36864]

    NT = [(0, 416), (416, 412), (828, 412), (1240, 412), (1652, 412)]
    FT32 = RR([View(f"wch{i}", WCHt[i][:].rearrange("p a b -> p (a b)").bitcast(F32), 0, 4, (512,))
               for i in (0, 1, 5)])
    SQW = RR([View(f"wch{i}", WCHt[i][:].rearrange("p a b -> p (a b)")[:, h * 512:(h + 1) * 512], h * 1024, 2, (512,))
              for i in (4,) for h in (0, 1)])

    def ffn_group(l, gi, do_load, do_compute, norm_goff=None, after_down=None, load_part=None, pre_tile=None):
        if True:
            (c0, npair, bi) = GROUPS[gi]
            off = GB_OFF[bi]
            wup_ap = big(off, BF16, (8, 2, npair * 128))
            wdn_off = off + 8 * 2 * npair * 128 * 2
            wdn_ap = big(wdn_off, BF16, (npair, 1024))

            def wup_acc(kc, gv, p_):
                lo = off + (((kc * 2 + gv) * npair + p_) * 128) * 2
                return Acc(wup_ap[:, kc, gv, p_ * 128:(p_ + 1) * 128], "BIG", [(lo, lo + 256)])

            def wdn_acc(p_, d):
                lo = wdn_off + (p_ * 1024 + d * 128) * 2
                return Acc(wdn_ap[:, p_, d * 128:(d + 1) * 128], "BIG", [(lo, lo + 256)])

            if do_load:
                k0, k1 = {None: (0, 8), "lo": (0, 4), "hi": (4, 8)}[load_part]
                for gv in range(2):
                    col0 = gv * DFF + c0 * 128
                    dst = Acc(wup_ap[:, k0:k1, gv, :], "BIG",
                              [(off + ((kc * 2 + gv) * npair * 128) * 2, off + ((kc * 2 + gv + 1) * npair * 128) * 2)
                               for kc in range(k0, k1)])
                    load_w(dst, w_up[l, k0 * 128:k1 * 128, col0:col0 + npair * 128].rearrange("(kc p) n -> p kc n", p=128))
                if load_part != "lo":
                    dst = Acc(wdn_ap, "BIG", [(wdn_off, wdn_off + npair * 1024 * 2)])
                    load_w(dst, w_down[l, c0 * 128:(c0 + npair) * 128, :].rearrange("(j p) n -> p j n", p=128))
            if not do_compute:
                return

            def up_pair(ti, p_):
                ms, mw, ho = FT[ti]
                if True:
                    tiles = []
                    for gv in range(2):
                        pb = PS8.next()
                        for kc in range(8):
                            matmul(pb((0, mw)), wup_acc(kc, gv, p_), HN(kc, (ms, ms + mw)), kc == 0, kc == 7)
                        ch = gv * 22 + c0 + p_
                        w0, w1, w2 = (vcol(O_FCW + (l * 3 + k) * 44 + ch) for k in range(3))
                        bcol = vcol(O_FCB + l * 44 + ch)
                        T = S2K.next()
                        a_pb, a_t = pb((0, mw)), T((0, mw))
                        P.act(lambda e, a_pb=a_pb, a_t=a_t, w2=w2, bcol=bcol: e.activation(
                            out=a_t.ap, in_=a_pb.ap, func=AF.Identity, scale=w2.ap, bias=bcol.ap),
                            reads=[a_pb, w2, bcol], writes=[a_t])
                        for sh, wcol in ((1, w1), (2, w0)):
                            a_in, a_o = pb((0, mw - sh)), T((sh, mw))
                            P.dve(lambda e, a_in=a_in, a_o=a_o, wcol=wcol: e.scalar_tensor_tensor(
                                out=a_o.ap, in0=a_in.ap, scalar=wcol.ap, in1=a_o.ap, op0=ALU.mult, op1=ALU.add),
                                reads=[a_in, wcol, a_o], writes=[a_o])
                        tiles.append(T)
                    TG, TV = tiles
                    a_g, a_v = TG((ho, mw)), TV((ho, mw))
                    P.act(lambda e, a_g=a_g: e.activation(out=a_g.ap, in_=a_g.ap, func=AF.Silu),
                          reads=[a_g], writes=[a_g])
                    at = S1K.next()
                    a_a = at((0, mw - ho))
                    (P.pool if p_ % 2 == 1 else P.dve)(lambda e, a_g=a_g, a_v=a_v, a_a=a_a: e.tensor_tensor(
                        out=a_a.ap, in0=a_g.ap, in1=a_v.ap, op=ALU.mult), reads=[a_g, a_v], writes=[a_a])
                    return a_a

            def down_d(ti, acts, d):
                ms, mw, ho = FT[ti]
                os_, ow = ms + ho, mw - ho
                if True:
                    pb = PS8.next()
                    for p_ in range(npair):
                        matmul(pb((0, ow)), wdn_acc(p_, d), acts[p_], p_ == 0, p_ == npair - 1)
                    if d % 4 == 0:
                        resid_add(d, os_, ow, pb((0, ow)))
                        return
                    a_pb, a_t, a_h = pb((0, ow)), FT32.next()((0, ow)), H(d, (os_, os_ + ow))
                    P.act(lambda e, a_pb=a_pb, a_t=a_t: e.activation(out=a_t.ap, in_=a_pb.ap, func=AF.Identity),
                          reads=[a_pb], writes=[a_t])
                    P.pool(lambda e, a_t=a_t, a_h=a_h: e.tensor_tensor(out=a_h.ap, in0=a_t.ap, in1=a_h.ap, op=ALU.add),
                           reads=[a_t, a_h], writes=[a_h])

            if norm_goff is not None:
                norm_tile(norm_goff, hn_dst, *NT[0], sq_pool=SQW)
                norm_tile(norm_goff, hn_dst, *NT[1], sq_pool=SQW)
            if pre_tile is not None:
                pre_tile(0)
                pre_tile(1)
            prev = [up_pair(0, p_) for p_ in range(npair)]
            for ti in range(len(FT)):
                if norm_goff is not None and ti + 2 < len(NT):
                    norm_tile(norm_goff, hn_dst, *NT[ti + 2], sq_pool=SQW)
                if pre_tile is not None and ti + 2 < len(NT):
                    pre_tile(ti + 2)
                nxt = []
                dq = list(range(NCH))
                if ti + 1 < len(FT):
                    for p_ in range(npair):
                        nxt.append(up_pair(ti + 1, p_))
                        ndo = (NCH * (p_ + 1)) // npair - (NCH * p_) // npair if p_ >= 1 else 0
                        for _ in range(ndo):
                            if dq:
                                down_d(ti, prev, dq.pop(0))
                while dq:
                    down_d(ti, prev, dq.pop(0))
                if after_down is not None:
                    after_down(ti)
                prev = nxt

    def mixer1():
        XB = []
        for i in range(2):
            base = 38400 + i * 16640
            XB.append([View("BIG", big(base + k * 8320, F32, (L + 16,)), base + k * 8320, 4, (L + 16,)) for k in range(2)])
        RS = [S2K.items[ti] for ti in range(len(NT))]
        fin = None
        for ti, (s, W) in enumerate(NT):
            nf = norm_stats(s, W, RS[ti]((0, W)), defer=True, sq_split=True)
            if fin is not None:
                fin()
            fin = nf
        fin()
        for i in range(2):
            a_z = XB[i][0]((0, 16))
            P.dve(lambda e, a_z=a_z: e.memset(a_z.ap, 0.0), writes=[a_z])
        pws = []
        for g in range(4):
            ti_, hf = 2 + g // 2, g % 2
            pw_ap = WCHt[ti_][:].rearrange("p a b -> p (a b)")[:, hf * 512:(hf + 1) * 512].rearrange(
                "p (kc n) -> p kc n", kc=2)
            load_w(Acc(pw_ap, f"wch{ti_}", [(hf * 1024, hf * 1024 + 1024)]),
                   pool_w[g].rearrange("(kc p) n -> p kc n", p=128))
            pws.append((f"wch{ti_}", pw_ap, hf * 1024))

        def pw_acc(g, kc, dc):
            space, pw_ap, base = pws[g]
            lo = base + (kc * 256 + dc * 128) * 2
            return Acc(pw_ap[:, kc, dc * 128:(dc + 1) * 128], space, [(lo, lo + 256)])

        def window(g, kc, part):
            wwin = 2 ** (g + 1)
            if True:
                c = 2 * g + kc
                slot = (g % 2) * 2 + kc
                X, Bb = XB[c % 2]
                if part == "act":
                    a_s, a_m = Bb((32, 16 + L)), HN(slot, (16, L))
                    P.act(lambda e, a_s=a_s, a_m=a_m: e.activation(
                        out=a_m.ap, in_=a_s.ap, func=AF.Identity, scale=1.0 / wwin), reads=[a_s], writes=[a_m])
                    a_xs, a_nx = X((16, 16 + L)), HN(4 + slot, (0, L))
                    P.act(lambda e, a_xs=a_xs, a_nx=a_nx: e.activation(out=a_nx.ap, in_=a_xs.ap, func=AF.Identity, scale=-1.0),
                          reads=[a_xs], writes=[a_nx])
                    return
                for ti, (s_, W_) in enumerate(NT):
                    a_h, a_g, a_r, a_x = H(c, (s_, s_ + W_)), vcol(O_MNO + c), RS[ti]((0, W_)), X((16 + s_, 16 + s_ + W_))
                    P.dve(lambda e, a_h=a_h, a_g=a_g, a_r=a_r, a_x=a_x: e.scalar_tensor_tensor(
                        out=a_x.ap, in0=a_h.ap, scalar=a_g.ap, in1=a_r.ap, op0=ALU.mult, op1=ALU.mult),
                        reads=[a_h, a_g, a_r], writes=[a_x])
                a_a, a_b, a_o = X((16, 16 + L)), X((16 - wwin, 16 - wwin + L)), Bb((16, 16 + L))
                if g == 0:
                    a_b1 = X((15, 15 + L))
                    P.dve(lambda e, a_a=a_a, a_b1=a_b1, a_o=a_o: e.tensor_tensor(
                        out=a_o.ap, in0=a_a.ap, in1=a_b1.ap, op=ALU.add), reads=[a_a, a_b1], writes=[a_o])
                else:
                    P.dve(lambda e, a_a=a_a, a_b=a_b, a_o=a_o: e.tensor_tensor_scan(
                        out=a_o.ap, data0=a_a.ap, data1=a_b.ap, initial=0.0, op0=ALU.add, op1=ALU.subtract),
                        reads=[a_a, a_b], writes=[a_o])
                a_s16, a_ic, a_m16 = Bb((16, 32)), VEC((O_INVC + g * 16, O_INVC + g * 16 + 16)), HN(slot, (0, 16))
                P.dve(lambda e, a_s16=a_s16, a_ic=a_ic, a_m16=a_m16: e.tensor_tensor(
                    out=a_m16.ap, in0=a_s16.ap, in1=a_ic.ap, op=ALU.mult), reads=[a_s16, a_ic], writes=[a_m16])

        def project(g, only_tile=None):
            for ti, (s, W) in enumerate(NT):
                if only_tile is not None and ti != only_tile:
                    continue
                for dc in range(2):
                    ch = 2 * g + dc
                    pb = PSG.next()
                    for kc in range(2):
                        matmul(pb((0, W)), pw_acc(g, kc, dc), HN((g % 2) * 2 + kc, (s, s + W)), kc == 0, False)
                    for kc in range(2):
                        matmul(pb((0, W)), pw_acc(g, kc, dc), HN(4 + (g % 2) * 2 + kc, (s, s + W)), False, kc == 1)
                    t = FT32.next()((0, W))
                    a_pb, a_sc, a_bs = pb((0, W)), vcol(O_POOLS + ch), vcol(O_PBS + ch)
                    P.act(lambda e, a_pb=a_pb, t=t, a_sc=a_sc, a_bs=a_bs: e.activation(
                        out=t.ap, in_=a_pb.ap, func=AF.Identity, scale=a_sc.ap, bias=a_bs.ap),
                        reads=[a_pb, a_sc, a_bs], writes=[t])
                    a_h = H(ch, (s, s + W))
                    (P.dve if (g == 3 and (ti + dc) % 2 == 0) else P.pool)(
                        lambda e, t=t, a_h=a_h: e.tensor_tensor(out=a_h.ap, in0=t.ap, in1=a_h.ap, op=ALU.add),
                        reads=[t, a_h], writes=[a_h])

        for kc in range(2):
            window(0, kc, "dve")
            window(0, kc, "act")
        for g in range(3):
            window(g + 1, 0, "dve")
            project(g)
            window(g + 1, 0, "act")
            window(g + 1, 1, "dve")
            window(g + 1, 1, "act")

        def tail_tile(ti):
            s, W = NT[ti]
            project(3, only_tile=ti)
            norm_tile(O_FFN1, hn_dst, s, W, sq_pool=SQW)
        mixer1_tail.append(tail_tile)

    final_outs = []

    def final_tile(ti):
        s, W = NT[ti]
        norm_tile(O_FIN, lambda c, s_, W_: H(c, (s_, s_ + W_)), s, W, sq_pool=SQW)
        a = max(s, NMETA)
        final_outs.append(P.dma("sp", outT[:, a - NMETA:s + W - NMETA].rearrange("(c p) t -> p c t", p=128),
                                Ht[:, :, a:s + W], reads=[H((0, NCH), (a, s + W))]))

    def final():
        P.fence("sp", final_outs)

    def ffn0():
        ffn_group(0, 1, True, False)
        ffn_group(0, 0, False, True)
        ffn_group(0, 2, True, False)
        ffn_group(0, 1, False, True)
        ffn_group(0, 3, True, False)
        ffn_group(0, 2, False, True)
        ffn_group(1, 0, True, False)
        ffn_group(0, 3, False, True)

    mixer1_tail = []

    def mixer1_and_prefetch():
        mixer1()
        ffn_group(1, 1, True, False)
        if stage == "mix1":
            for ti in range(len(NT)):
                mixer1_tail[0](ti)

    def ffn1():
        ffn_group(1, 0, False, True, pre_tile=mixer1_tail[0])
        ffn_group(1, 2, True, False)
        ffn_group(1, 1, False, True)
        ffn_group(1, 3, True, False)
        ffn_group(1, 2, False, True)
        ffn_group(1, 3, False, True, after_down=final_tile)

    stages = [("mix0", mixer0), ("ffn0", ffn0), ("mix1", mixer1_and_prefetch), ("ffn1", ffn1)]
    done = False
    for name, fn in stages:
        fn()
        if stage == name:
            dump_and_finish()
            done = True
            break
    if not done:
        final()
        if stage is not None:
            dump_and_finish()

    sems = {e: es.enter_context(nc.semaphore(f"s_{e}")) for e in ("pe", "act", "dve", "pool")}
    dma_sems = {q: [es.enter_context(nc.semaphore(f"d_{q}{i}")) for i in range(NDMASEM)] for q in ("sp", "pool")}
    P.finalize(sems, dma_sems)
    with nc.Block() as block:
        @block.sync
        def _(e):
            P.emit("sp", e)

        @block.gpsimd
        def _(e):
            P.emit("pool", e)

        @block.tensor
        def _(e):
            P.emit("pe", e)

        @block.scalar
        def _(e):
            P.emit("act", e)

        @block.vector
        def _(e):
            P.emit("dve", e)
    es.close()
    return nc


def _cols(v):
    v = np.asarray(v, np.float32).reshape(-1, 128)
    return np.ascontiguousarray(v.T)


def pack_vecs(inp):
    V = np.zeros((128, NVH), np.float32)
    V[:, O_MNE:O_MNE + 8] = _cols(inp["mix_norm_even"][0])
    V[:, O_MNO:O_MNO + 8] = _cols(inp["mix_norm_odd"][0])
    V[:, O_FFN0:O_FFN0 + 8] = _cols(inp["ffn_norm"][0])
    V[:, O_FFN1:O_FFN1 + 8] = _cols(inp["ffn_norm"][1])
    V[:, O_FIN:O_FIN + 8] = _cols(inp["final_norm"])
    cw = np.asarray(inp["conv_w"][0], np.float32)
    V[:, O_CONVW:O_CONVW + 124] = cw.reshape(31, 4, 128).transpose(2, 1, 0).reshape(128, 124)
    V[:, O_CONVB:O_CONVB + 4] = _cols(inp["conv_b"][0])
    V[:, O_LNG:O_LNG + 4] = _cols(inp["ln_g"][0])
    V[:, O_LNB:O_LNB + 4] = _cols(inp["ln_b"][0])
    V[:, O_POOLB:O_POOLB + 8] = _cols(np.asarray(inp["pool_b"][0]).reshape(-1))
    V[:, O_POOLS:O_POOLS + 8] = _cols(inp["pool_scale"][0])
    fw = np.asarray(inp["ffn_conv_w"], np.float32)
    V[:, O_FCW:O_FCW + 264] = fw.reshape(2, 3, 44, 128).transpose(3, 0, 1, 2).reshape(128, 264)
    fb = np.asarray(inp["ffn_conv_b"], np.float32)
    V[:, O_FCB:O_FCB + 88] = fb.reshape(2, 44, 128).transpose(2, 0, 1).reshape(128, 88)
    V[0:8, O_BF] = np.asarray(inp["b_f"][0], np.float32)
    for g, w in enumerate((2, 4, 8, 16)):
        V[:, O_INVC + g * 16:O_INVC + (g + 1) * 16] = (1.0 / np.minimum(np.arange(1, 17), w)).astype(np.float32)[None, :]
    return V


def make_in_maps(inp, cores):
    x = np.asarray(inp["x"], np.float32)
    vecs = pack_vecs(inp)
    shared = {
        "metaT": np.ascontiguousarray(np.asarray(inp["meta_tokens"], np.float32).T),
        "vecs": vecs,
        "w_in": np.ascontiguousarray(np.asarray(inp["w_in"][0], np.float32)),
        "w_out": np.ascontiguousarray(np.asarray(inp["w_out"][0], np.float32)),
        "pool_w": np.ascontiguousarray(np.asarray(inp["pool_w"][0], np.float32)),
        "w_up": np.ascontiguousarray(np.asarray(inp["w_up"], np.float32)),
        "w_down": np.ascontiguousarray(np.asarray(inp["w_down"], np.float32)),
    }
    maps = []
    for b in cores:
        m = dict(shared)
        m["xT"] = np.ascontiguousarray(x[b].T)
        maps.append(m)
    return maps


_NC_CACHE = {}


def kernel(**inputs):
    if "full" not in _NC_CACHE:
        _NC_CACHE["full"] = build_program(None)
    nc = _NC_CACHE["full"]
    n = 8
    in_maps = make_in_maps(inputs, list(range(n)))
    res = run_bass_kernel_spmd(nc, in_maps, core_ids=list(range(n)))
    out = np.empty((n, SEQ, D), np.float32)
    for b in range(n):
        out[b] = res.results[b]["outT"].T
    return out
```

```python
import os
import numpy as np
import concourse.bass as bass
import concourse.mybir as mybir
from concourse.bass_utils import run_bass_kernel_spmd
from contextlib import ExitStack

F32 = mybir.dt.float32
BF16 = mybir.dt.bfloat16
ALU = mybir.AluOpType
AF = mybir.ActivationFunctionType

D = 1024
L = 2064
NMETA = 16
SEQ = 2048
DFF = 2816
NCH = 8
TT = [(0, 16), (16, 512), (528, 512), (1040, 512), (1552, 512)]
KB = [(0, 16)] + [(16 + 128 * j, 128) for j in range(16)]
PT = [(0, 413), (413, 413), (826, 413), (1239, 413), (1652, 412)]
FT = [(0, 416, 0), (414, 414, 2), (826, 414, 2), (1238, 414, 2), (1650, 414, 2)]
GROUPS = [(0, 6, 0), (6, 5, 1), (11, 6, 0), (17, 5, 1)]

O_MNE, O_MNO, O_FFN0, O_FFN1, O_FIN = 0, 8, 16, 24, 32
O_CONVW, O_CONVB, O_LNG, O_LNB = 40, 164, 168, 172
O_POOLB, O_POOLS = 176, 184
O_FCW, O_FCB = 192, 456
O_BF = 544
O_INVC = 545
NVH = 609
O_NBF = 609
O_PBS = 610
NV = 618


class Acc:
    __slots__ = ("ap", "space", "ranges")

    def __init__(self, ap, space, ranges):
        self.ap = ap
        self.space = space
        self.ranges = ranges


class View:
    def __init__(self, space, ap, off, esize, dims):
        self.space, self.ap, self.off, self.esize, self.dims = space, ap, off, esize, tuple(dims)

    def __call__(self, *idx, p=None):
        ps = slice(None) if p is None else slice(p[0], p[1])
        es, off = self.esize, self.off
        if len(self.dims) == 1:
            (a, b), = idx if idx else ((0, self.dims[0]),)
            return Acc(self.ap[ps, a:b], self.space, [(off + a * es, off + b * es)])
        R, C = self.dims
        r = idx[0]
        a, b = idx[1] if len(idx) > 1 else (0, C)
        if isinstance(r, int):
            return Acc(self.ap[ps, r, a:b], self.space, [(off + (r * C + a) * es, off + (r * C + b) * es)])
        r0, r1 = r
        return Acc(self.ap[ps, r0:r1, a:b], self.space,
                   [(off + (q * C + a) * es, off + (q * C + b) * es) for q in range(r0, r1)])

    def whole(self):
        n = int(np.prod(self.dims))
        return Acc(self.ap, self.space, [(self.off, self.off + n * self.esize)])


class Op:
    __slots__ = ("eng", "fn", "deps", "is_dma", "token", "signal", "uid", "prev_same_sem")

    def __init__(self, eng, fn, is_dma, uid):
        self.eng, self.fn, self.is_dma, self.uid = eng, fn, is_dma, uid
        self.deps = set()
        self.token = None
        self.signal = False
        self.prev_same_sem = None


class Tracker:
    def __init__(self):
        self.spaces = {}

    def _segs(self, space):
        if space not in self.spaces:
            self.spaces[space] = [[0, 1 << 40, None, {}]]
        return self.spaces[space]

    def access(self, op, space, lo, hi, write):
        segs = self._segs(space)
        out = []
        deps = op.deps
        for seg in segs:
            s_lo, s_hi, w, rd = seg
            if s_hi <= lo or s_lo >= hi:
                out.append(seg)
                continue
            if s_lo < lo:
                out.append([s_lo, lo, w, dict(rd)])
                s_lo = lo
            tail = None
            if s_hi > hi:
                tail = [hi, s_hi, w, dict(rd)]
                s_hi = hi
            if w is not None and w is not op:
                deps.add(w)
            if write:
                for r in rd.values():
                    if r is not op:
                        deps.add(r)
                out.append([s_lo, s_hi, op, {}])
            else:
                rd = dict(rd)
                key = ("d", op.uid) if op.is_dma else op.eng
                rd[key] = op
                out.append([s_lo, s_hi, w, rd])
            if tail is not None:
                out.append(tail)
        merged = []
        for seg in out:
            if merged and merged[-1][2] is seg[2] and merged[-1][3] == seg[3] and merged[-1][1] == seg[0]:
                merged[-1][1] = seg[1]
            else:
                merged.append(seg)
        self.spaces[space] = merged


ENGS = ("pe", "act", "dve", "pool", "sp")
NDMASEM = 8


class Prog:
    def __init__(self):
        self.q = {e: [] for e in ENGS}
        self.tr = Tracker()
        self.uid = 0

    def add(self, eng, fn, reads=(), writes=(), dma=False):
        self.uid += 1
        op = Op(eng, fn, dma, self.uid)
        for a in reads:
            for lo, hi in a.ranges:
                self.tr.access(op, a.space, lo, hi, False)
        for a in writes:
            for lo, hi in a.ranges:
                self.tr.access(op, a.space, lo, hi, True)
        self.q[eng].append(op)
        return op

    def pe(self, fn, reads=(), writes=()):
        return self.add("pe", fn, reads, writes)

    def act(self, fn, reads=(), writes=()):
        return self.add("act", fn, reads, writes)

    def dve(self, fn, reads=(), writes=()):
        return self.add("dve", fn, reads, writes)

    def pool(self, fn, reads=(), writes=()):
        return self.add("pool", fn, reads, writes)

    def dma(self, queue, out, in_, reads=(), writes=()):
        return self.add(queue, lambda e: e.dma_start(out=out, in_=in_), reads, writes, dma=True)

    def fence(self, queue, ops):
        self.uid += 1
        op = Op(queue, None, False, self.uid)
        op.deps = set(ops)
        self.q[queue].append(op)
        return op

    def finalize(self, sems, dma_sems):
        for e in ENGS:
            for op in self.q[e]:
                for d in op.deps:
                    d.signal = True
        for e in ENGS:
            cnt = 0
            ndma = 0
            last_on_sem = {}
            for op in self.q[e]:
                if op.is_dma:
                    si = ndma % NDMASEM
                    sem = dma_sems[e][si]
                    op.prev_same_sem = last_on_sem.get(si)
                    op.token = (sem, 16 * (ndma // NDMASEM + 1))
                    last_on_sem[si] = op
                    op.signal = True
                    ndma += 1
                elif op.signal:
                    cnt += 1
                    op.token = (sems[e], cnt)

    def emit(self, eng_name, e):
        known = {}
        for op in self.q[eng_name]:
            waits = {}
            deps = list(op.deps)
            if op.prev_same_sem is not None:
                deps.append(op.prev_same_sem)
            for d in deps:
                if eng_name == "pe" and d.eng == "pe" and not d.is_dma:
                    continue
                sem, val = d.token
                k = id(sem)
                if k not in waits or waits[k][1] < val:
                    waits[k] = (sem, val)
            for k, (sem, val) in waits.items():
                if known.get(k, 0) < val:
                    e.wait_ge(sem, val)
                    known[k] = val
            if op.fn is None:
                continue
            ins = op.fn(e)
            if op.signal:
                sem, val = op.token
                ins.then_inc(sem, 16 if op.is_dma else 1)


class RR:
    def __init__(self, items):
        self.items, self.i = items, 0

    def next(self):
        v = self.items[self.i % len(self.items)]
        self.i += 1
        return v


def build_program(stage=None):
    nc = bass.Bass("TRN2", target_bir_lowering=False)
    P = Prog()
    es = ExitStack()

    def dram(name, shape, kind="ExternalInput"):
        return nc.dram_tensor(name, shape, F32, kind=kind).ap()

    xT = dram("xT", [D, SEQ])
    metaT = dram("metaT", [D, NMETA])
    vecs_d = dram("vecs", [128, NVH])
    w_in = dram("w_in", [D, 2568])
    w_out = dram("w_out", [D, D])
    pool_w = dram("pool_w", [4, 256, 256])
    w_up = dram("w_up", [2, D, 2 * DFF])
    w_down = dram("w_down", [2, DFF, D])
    outT = dram("outT", [D, SEQ], kind="ExternalOutput")
    dbg = dram("dbg", [128, NCH * L], kind="ExternalOutput") if stage is not None else None
    c3d = nc.dram_tensor("c3d", [8, 3, L], BF16, kind="Internal").ap()

    def sb(name, shape, dt):
        return es.enter_context(nc.sbuf_tensor(name, shape, dt))

    Ht = sb("H", [128, NCH, L], F32)
    HNt = sb("HN", [128, NCH * L], BF16)
    BIGt = sb("BIG", [128, 35840], BF16)
    VECt = sb("VEC", [128, NV], F32)
    IDt = sb("ident", [128, 128], BF16)
    TRIt = sb("tri", [128, 128], BF16)
    ONEt = sb("ones", [128, 128], BF16)
    WFt = sb("wf", [128, 8, 8], BF16)
    WCHt = [sb(f"wch{i}", [128, 8, 128], BF16) for i in range(6)]
    S2Kt = [sb(f"s2k{i}", [128, 512], F32) for i in range(6)]
    S1Kt = [sb(f"s1k{i}", [128, 512], BF16) for i in range(12)]
    PSt = [es.enter_context(nc.psum_tensor(f"ps{i}", [128, 512], F32)) for i in range(8)]

    H = View("H", Ht[:], 0, 4, (NCH, L))
    HN = View("HN", HNt[:].rearrange("p (c t) -> p c t", c=NCH), 0, 2, (NCH, L))
    Y = View("HN", HNt[:].bitcast(F32).rearrange("p (c t) -> p c t", c=4), 0, 4, (4, L))
    VEC = View("VEC", VECt[:], 0, 4, (NV,))
    IDN = View("ident", IDt[:], 0, 2, (128,))
    TRI = View("tri", TRIt[:], 0, 2, (128,))
    ONE = View("ones", ONEt[:], 0, 2, (128,))
    WF = View("wf", WFt[:], 0, 2, (8, 8))
    WCH = RR([View(f"wch{i}", t[:], 0, 2, (8, 128)) for i, t in enumerate(WCHt)])
    S2K = RR([View(f"s2k{i}", t[:], 0, 4, (512,)) for i, t in enumerate(S2Kt)])
    S1K = RR([View(f"s1k{i}", t[:], 0, 2, (512,)) for i, t in enumerate(S1Kt)])
    PSG = RR([View(f"ps{i}", t[:], 0, 4, (512,)) for i, t in enumerate(PSt[:5])])
    PSA = RR([View(f"ps{i}", t[:], 0, 4, (512,)) for i, t in enumerate(PSt[5:], start=5)])
    S2K4 = RR(S2K.items[:4])
    PS8 = RR([View(f"ps{i}", t[:], 0, 4, (512,)) for i, t in enumerate(PSt)])

    def big(off, dt, dims):
        esz = 4 if dt == F32 else 2
        n = int(np.prod(dims))
        ap = BIGt[:, off // 2: off // 2 + n * esz // 2]
        if dt == F32:
            ap = ap.bitcast(F32)
        if len(dims) == 2:
            ap = ap.rearrange("p (a b) -> p a b", a=dims[0])
        elif len(dims) == 3:
            ap = ap.rearrange("p (a b c) -> p a b c", a=dims[0], b=dims[1])
        return ap

    def vcol(j, p=None):
        return VEC((j, j + 1), p=p)

    dbg_ops = []

    def dump_and_finish():
        o = P.dma("sp", dbg, Ht[:].rearrange("p c t -> p (c t)"), reads=[H.whole()])
        P.fence("sp", [o])

    P.dma("sp", VECt[:, 0:NVH], vecs_d, writes=[VEC((0, NVH))])
    P.dma("sp", Ht[:, :, 0:NMETA], metaT.rearrange("(c p) t -> p c t", p=128),
          writes=[H((0, NCH), (0, NMETA))])
    for (s, W) in PT:
        a = max(s, NMETA)
        P.dma("sp", Ht[:, :, a:s + W], xT[:, a - NMETA:s + W - NMETA].rearrange("(c p) t -> p c t", p=128),
              writes=[H((0, NCH), (a, s + W))])
    P.pool(lambda e: e.memset(ONEt[:], 1.0), writes=[ONE.whole()])
    P.pool(lambda e: e.memset(IDt[:], 1.0), writes=[IDN.whole()])
    P.pool(lambda e: e.affine_select(out=IDt[:], in_=IDt[:], pattern=[[-1, 128]], compare_op=ALU.is_equal,
                                     fill=0.0, base=0, channel_multiplier=1),
           reads=[IDN.whole()], writes=[IDN.whole()])
    P.pool(lambda e: e.memset(TRIt[:], 1.0), writes=[TRI.whole()])
    P.pool(lambda e: e.affine_select(out=TRIt[:], in_=TRIt[:], pattern=[[1, 128]], compare_op=ALU.is_ge,
                                     fill=0.0, base=0, channel_multiplier=-1),
           reads=[TRI.whole()], writes=[TRI.whole()])
    P.dve(lambda e: e.tensor_scalar(out=VECt[:, O_NBF:O_NBF + 1], in0=VECt[:, O_BF:O_BF + 1], scalar1=-1.0,
                                    scalar2=None, op0=ALU.mult),
          reads=[vcol(O_BF)], writes=[vcol(O_NBF)])
    P.dve(lambda e: e.tensor_tensor(out=VECt[:, O_PBS:O_PBS + 8], in0=VECt[:, O_POOLB:O_POOLB + 8],
                                    in1=VECt[:, O_POOLS:O_POOLS + 8], op=ALU.mult),
          reads=[VEC((O_POOLB, O_POOLS + 8))], writes=[VEC((O_PBS, O_PBS + 8))])

    def matmul(out, lhsT, rhs, start, stop):
        P.pe(lambda e: e.matmul(out.ap, lhsT.ap, rhs.ap, start=start, stop=stop),
             reads=[lhsT, rhs], writes=[out])

    def load_w(dst_acc, src_ap):
        P.dma("pool", dst_acc.ap, src_ap, writes=[dst_acc])

    def norm_stats(s, W, out_acc, src=H, nrows=NCH, scale=1.0 / D, eps=1e-6, sq_pool=None, defer=False, sq_split=False):
        pb = PSG.next()
        for c in range(nrows):
            sq = (sq_pool or S1K).next()
            a_in, a_sq = src(c, (s, s + W)), sq((0, W))
            if sq_split and c % 2 == 1:
                P.dve(lambda e, a_in=a_in, a_sq=a_sq: e.tensor_tensor(out=a_sq.ap, in0=a_in.ap, in1=a_in.ap, op=ALU.mult),
                      reads=[a_in], writes=[a_sq])
            else:
                P.act(lambda e, a_in=a_in, a_sq=a_sq: e.activation(out=a_sq.ap, in_=a_in.ap, func=AF.Square),
                      reads=[a_in], writes=[a_sq])
            matmul(pb((0, W)), ONE.whole(), a_sq, c == 0, c == nrows - 1)
        a_pb = pb((0, W))
        if defer:
            return lambda: norm_fin(a_pb, out_acc, scale, eps)
        norm_fin(a_pb, out_acc, scale, eps)

    def norm_fin(a_pb, out_acc, scale, eps):
        P.act(lambda e: e.activation(out=out_acc.ap, in_=a_pb.ap, func=AF.Ln, scale=scale, bias=eps),
              reads=[a_pb], writes=[out_acc])
        P.act(lambda e: e.activation(out=out_acc.ap, in_=out_acc.ap, func=AF.Exp, scale=-0.5),
              reads=[out_acc], writes=[out_acc])

    def norm_tile(goff, dst, s, W, sq_pool=None, r_pool=None):
        r = (r_pool or S2K).next()((0, W))
        norm_stats(s, W, r, sq_pool=sq_pool)
        for c in range(NCH):
            a_h, a_g, a_o = H(c, (s, s + W)), vcol(goff + c), dst(c, s, W)
            P.dve(lambda e, a_h=a_h, a_g=a_g, a_o=a_o, r=r: e.scalar_tensor_tensor(
                out=a_o.ap, in0=a_h.ap, scalar=a_g.ap, in1=r.ap, op0=ALU.mult, op1=ALU.mult),
                reads=[a_h, a_g, r], writes=[a_o])

    def rmsnorm(goff, dst, after_tile=None):
        for (s, W) in PT:
            norm_tile(goff, dst, s, W)
            if after_tile is not None:
                after_tile(s, W)

    def hn_dst(c, s, W):
        return HN(c, (s, s + W))

    def resid_add(d, s, W, pb_acc):
        a_h = H(d, (s, s + W))
        P.dve(lambda e: e.tensor_tensor(out=a_h.ap, in0=pb_acc.ap, in1=a_h.ap, op=ALU.add),
              reads=[pb_acc, a_h], writes=[a_h])

    def mixer0():
        VAS = [big(i * 8704, BF16, (17, 2, 128)) for i in range(2)]
        QKS = [[View("BIG", big(17408 + b_ * 16512 + i * 4128, BF16, (L,)), 17408 + b_ * 16512 + i * 4128, 2, (L,))
                for i in range(4)] for b_ in range(2)]
        C3 = View("BIG", big(33920, BF16, (3, L)), 33920, 2, (3, L))
        T1 = View("BIG", big(55168, F32, (L,)), 55168, 4, (L,))
        T2 = View("BIG", big(63424, F32, (L,)), 63424, 4, (L,))
        CATLO = View("BIG", big(55168, BF16, (4, L)), 55168, 2, (4, L))
        U = View("BIG", big(0, BF16, (4, L + 30)), 0, 2, (4, L + 30))
        CATHI = View("BIG", big(36864, BF16, (4, L)), 36864, 2, (4, L))
        DG = [View("BIG", big(16768 + i * 7936, BF16, (31, 128)), 16768 + i * 7936, 2, (31, 128)) for i in range(2)]

        def va_acc(vb, j, hh, c0, c1, kw):
            lo = vb * 8704 + ((j * 2 + hh) * 128 + c0) * 2
            hi = vb * 8704 + ((j * 2 + hh) * 128 + c1) * 2
            return Acc(VAS[vb][0:kw, j, hh, c0:c1], "BIG", [(lo, hi)])

        def proj_pieces(pr):
            bsel = pr % 2
            qA, qB, kA, kB = QKS[bsel]
            w3 = []

            def loads():
                wq, wk, wv = WCH.next(), WCH.next(), WCH.next()
                load_w(wq.whole(), w_in[:, pr * 128:(pr + 1) * 128].rearrange("(kc p) n -> p kc n", p=128))
                load_w(wk.whole(), w_in[:, 512 + pr * 128:512 + (pr + 1) * 128].rearrange("(kc p) n -> p kc n", p=128))
                load_w(wv.whole(), w_in[:, 1024 + pr * 128:1024 + (pr + 1) * 128].rearrange("(kc p) n -> p kc n", p=128))
                w3.extend([wq, wk, wv])
            pieces = [loads]
            for which in range(2):
                tA, tB = (qA, qB) if which == 0 else (kA, kB)
                scl, fillv, row0 = (0.125, 1.0, 64) if which == 0 else (1.0, -1.0, 67)

                def ptile(s, W, which=which, tA=tA, tB=tB, scl=scl):
                    wt = w3[which]
                    pb = PSG.next()
                    for kc in range(8):
                        matmul(pb((0, W)), wt(kc), HN(kc, (s, s + W)), kc == 0, kc == 7)
                    for hh, tX in enumerate((tA, tB)):
                        a_pb, a_o = pb((0, W), p=(hh * 64, hh * 64 + 64)), tX((s, s + W), p=(0, 64))
                        P.dve(lambda e, a_pb=a_pb, a_o=a_o: e.tensor_scalar(
                            out=a_o.ap, in0=a_pb.ap, scalar1=scl, scalar2=None, op0=ALU.mult),
                            reads=[a_pb], writes=[a_o])
                for (s, W) in PT:
                    pieces.append(lambda s=s, W=W, ptile=ptile: ptile(s, W))

                def post(tA=tA, tB=tB, fillv=fillv, row0=row0):
                    for hh, tX in enumerate((tA, tB)):
                        a_r = tX((0, L), p=(row0, row0 + 3))
                        P.dma("sp", a_r.ap, c3d[2 * pr + hh], reads=[c3d_acc], writes=[a_r])
                pieces.append(post)

            def vblock(j):
                ks, kw = KB[j]
                wv = w3[2]
                pb = PSG.next()
                for kc in range(8):
                    matmul(pb((0, 128), p=(0, kw)), HN(kc, (ks, ks + kw)), wv(kc), kc == 0, kc == 7)
                a_pb = pb((0, 128), p=(0, kw))
                a_pb3 = Acc(a_pb.ap.rearrange("p (h d) -> p h d", h=2), a_pb.space, a_pb.ranges)
                a_v = Acc(VAS[bsel][0:kw, j, :, 0:64], "BIG", [(bsel * 8704 + j * 512, bsel * 8704 + j * 512 + 512)])
                P.dve(lambda e: e.tensor_copy(out=a_v.ap, in_=a_pb3.ap), reads=[a_pb3], writes=[a_v])
            for j in range(len(KB)):
                pieces.append(lambda j=j: vblock(j))
            return pieces

        def core(pr, nxt_pieces):
            bsel = pr % 2
            qA, qB, kA, kB = QKS[bsel]
            step = 0
            pending_norm = []
            for hh in range(2):
                qa, ka = (qA, kA) if hh == 0 else (qB, kB)
                for qi, (qs, qw) in enumerate(TT):
                    if qi == 0:
                        blocks = [(0, 0, 16, True)]
                    else:
                        nfull = 4 * (qi - 1)
                        blocks = [(j, 0, 512, False) for j in range(0, nfull + 1)]
                        blocks += [(nfull + 1 + m, 128 * m, 512 - 128 * m, True) for m in range(4)]
                    po = PSA.next()
                    nb = len(blocks)
                    LOOK = 3
                    ptiles = [None] * nb
                    for idx in range(nb + LOOK):
                        if idx < nb:
                            j, off, n, diag = blocks[idx]
                            ks, kw = KB[j]
                            ps_ = PSG.next()
                            a_s = ps_((0, n), p=(0, kw))
                            matmul(a_s, ka((ks, ks + kw), p=(0, 70)), qa((qs + off, qs + off + n), p=(0, 70)), True, True)
                            pt = S1K.next()
                            a_p = pt((0, n), p=(0, kw))
                            P.act(lambda e, a_s=a_s, a_p=a_p: e.activation(out=a_p.ap, in_=a_s.ap, func=AF.Exp),
                                  reads=[a_s], writes=[a_p])
                            if diag:
                                mw = min(128, n)
                                a_pm, a_tr = pt((0, mw), p=(0, kw)), TRI((0, mw), p=(0, kw))
                                P.dve(lambda e, a_pm=a_pm, a_tr=a_tr: e.tensor_tensor(
                                    out=a_pm.ap, in0=a_pm.ap, in1=a_tr.ap, op=ALU.mult),
                                    reads=[a_pm, a_tr], writes=[a_pm])
                            ptiles[idx] = a_p
                            if pending_norm and idx == min(1, nb - 1):
                                pending_norm.pop(0)()
                        k2 = idx - LOOK
                        if k2 >= 0:
                            j, off, n, diag = blocks[k2]
                            ks, kw = KB[j]
                            matmul(po((off, off + n)), va_acc(bsel, j, hh, 0, 128, kw), ptiles[k2], k2 == 0, k2 == nb - 1)
                            step += 1
                            if nxt_pieces and step % 3 == 0:
                                nxt_pieces.pop(0)()
                    def normalize(po=po, qs=qs, qw=qw, hh=hh):
                        rb = S2K.next()
                        a_den, a_rb = po((0, qw), p=(64, 128)), rb((0, qw), p=(0, 64))
                        if qw > 64:
                            P.dve(lambda e: e.reciprocal(out=a_rb.ap, in_=a_den.ap), reads=[a_den], writes=[a_rb])
                        else:
                            P.act(lambda e: e.activation(out=a_rb.ap, in_=a_den.ap, func=AF.Ln), reads=[a_den], writes=[a_rb])
                            P.act(lambda e: e.activation(out=a_rb.ap, in_=a_rb.ap, func=AF.Exp, scale=-1.0),
                                  reads=[a_rb], writes=[a_rb])
                        a_num = po((0, qw), p=(0, 64))
                        a_cat = CATLO(pr, (qs, qs + qw), p=(hh * 64, hh * 64 + 64))
                        P.dve(lambda e: e.tensor_tensor(out=a_cat.ap, in0=a_num.ap, in1=a_rb.ap, op=ALU.mult),
                              reads=[a_num, a_rb], writes=[a_cat])
                    pending_norm.append(normalize)
            while pending_norm:
                pending_norm.pop(0)()
            while nxt_pieces:
                nxt_pieces.pop(0)()

        c3d_acc = Acc(c3d, "C3D", [(0, 8 * 3 * L * 2)])
        pieces0 = proj_pieces(0)
        load_w(WF.whole(), w_in[:, 1536:1544].rearrange("(kc p) n -> p kc n", p=128))
        pieces0[0]()
        for vb in range(2):
            va_all = Acc(VAS[vb][:, :, :, 64:128], "BIG", [(vb * 8704, vb * 8704 + 8704)])
            P.pool(lambda e, va_all=va_all: e.memset(va_all.ap, 1.0), reads=[], writes=[va_all])
        for b_ in range(2):
            for i, fv in enumerate((1.0, 1.0, -1.0, -1.0)):
                a_aug = QKS[b_][i]((0, L), p=(64, 70))
                P.pool(lambda e, a_aug=a_aug, fv=fv: e.memset(a_aug.ap, fv), writes=[a_aug])

        q_t, q_post, k_t, k_post, v_b = pieces0[1:6], pieces0[6], pieces0[7:12], pieces0[12], pieces0[13:]
        norm_tile(O_MNE, hn_dst, *PT[0])
        for i, (s, W) in enumerate(PT):
            if i + 1 < len(PT):
                norm_tile(O_MNE, hn_dst, *PT[i + 1])
            pb = PSG.next()
            for kc in range(8):
                matmul(pb((0, W), p=(0, 8)), WF(kc), HN(kc, (s, s + W)), kc == 0, kc == 7)
            a_pb, a_t, a_b = pb((0, W), p=(0, 8)), T1((s, s + W), p=(0, 8)), vcol(O_NBF, p=(0, 8))
            P.act(lambda e, a_pb=a_pb, a_t=a_t, a_b=a_b: e.activation(out=a_t.ap, in_=a_pb.ap, func=AF.Exp,
                                                                     scale=-1.0, bias=a_b.ap),
                  reads=[a_pb, a_b], writes=[a_t])
            t1i, t2i = T1((s, s + W), p=(0, 8)), T2((s, s + W), p=(0, 8))
            P.act(lambda e, t1i=t1i: e.activation(out=t1i.ap, in_=t1i.ap, func=AF.Ln, bias=1.0),
                  reads=[t1i], writes=[t1i])
            if i == 0:
                P.dve(lambda e, t1i=t1i, t2i=t2i: e.tensor_tensor_scan(
                    out=t2i.ap, data0=t1i.ap, data1=t1i.ap, initial=0.0, op0=ALU.add, op1=ALU.max),
                    reads=[t1i], writes=[t2i])
            else:
                a_init = T2((s - 1, s), p=(0, 8))
                P.dve(lambda e, t1i=t1i, t2i=t2i, a_init=a_init: e.tensor_tensor_scan(
                    out=t2i.ap, data0=t1i.ap, data1=t1i.ap, initial=a_init.ap, op0=ALU.add, op1=ALU.max),
                    reads=[t1i, a_init], writes=[t2i])
            c0i, c1i, c2i = (C3(j, (s, s + W), p=(0, 8)) for j in range(3))
            P.pool(lambda e, c0i=c0i, t2i=t2i: e.tensor_copy(out=c0i.ap, in_=t2i.ap), reads=[t2i], writes=[c0i])
            P.pool(lambda e, t1i=t1i, t2i=t2i, c0i=c0i: e.tensor_tensor(out=t1i.ap, in0=t2i.ap, in1=c0i.ap, op=ALU.subtract),
                   reads=[t2i, c0i], writes=[t1i])
            P.pool(lambda e, c1i=c1i, t1i=t1i: e.tensor_copy(out=c1i.ap, in_=t1i.ap), reads=[t1i], writes=[c1i])
            P.pool(lambda e, t1i=t1i, c1i=c1i: e.tensor_tensor(out=t1i.ap, in0=t1i.ap, in1=c1i.ap, op=ALU.subtract),
                   reads=[t1i, c1i], writes=[t1i])
            P.pool(lambda e, c2i=c2i, t1i=t1i: e.tensor_copy(out=c2i.ap, in_=t1i.ap), reads=[t1i], writes=[c2i])
            q_t[i]()
            k_t[i]()
        for piece in v_b:
            piece()
        c3_all = C3((0, 3), (0, L), p=(0, 8))
        P.dma("sp", c3d, c3_all.ap, reads=[c3_all], writes=[c3d_acc])
        q_post()
        k_post()
        for pr in range(4):
            nxt = proj_pieces(pr + 1) if pr + 1 < 4 else []
            core(pr, nxt)

        a_pad = U((0, 4), (0, 30))
        P.dve(lambda e: e.memset(a_pad.ap, 0.0), writes=[a_pad])
        for m in range(4):
            wa, wg = WCH.next(), WCH.next()
            load_w(wa.whole(), w_in[:, 1544 + m * 128:1544 + (m + 1) * 128].rearrange("(kc p) n -> p kc n", p=128))
            load_w(wg.whole(), w_in[:, 2056 + m * 128:2056 + (m + 1) * 128].rearrange("(kc p) n -> p kc n", p=128))
            for (s, W) in PT:
                pa, pg = PSG.next(), PSG.next()
                for kc in range(8):
                    matmul(pa((0, W)), wa(kc), HN(kc, (s, s + W)), kc == 0, kc == 7)
                for kc in range(8):
                    matmul(pg((0, W)), wg(kc), HN(kc, (s, s + W)), kc == 0, kc == 7)
                sg = S2K.next()
                a_pg, a_sg, a_pa, a_u = pg((0, W)), sg((0, W)), pa((0, W)), U(m, (30 + s, 30 + s + W))
                P.act(lambda e, a_pg=a_pg, a_sg=a_sg: e.activation(out=a_sg.ap, in_=a_pg.ap, func=AF.Sigmoid),
                      reads=[a_pg], writes=[a_sg])
                P.dve(lambda e, a_pa=a_pa, a_sg=a_sg, a_u=a_u: e.tensor_tensor(
                    out=a_u.ap, in0=a_pa.ap, in1=a_sg.ap, op=ALU.mult), reads=[a_pa, a_sg], writes=[a_u])
        def conv_tile(m, s, W):
            dg = DG[(m + 1) % 2]
            npe = 25 if m < 3 else 31
            pb = PSG.next()
            for k in range(npe):
                matmul(pb((0, W)), dg(k), U(m, (s + k, s + k + W)), k == 0, k == npe - 1)
            a_pb, a_y, a_b = pb((0, W)), Y(m, (s, s + W)), vcol(O_CONVB + m)
            P.act(lambda e: e.activation(out=a_y.ap, in_=a_pb.ap, func=AF.Identity, bias=a_b.ap),
                  reads=[a_pb, a_b], writes=[a_y])
            for k in range(npe, 31):
                a_u, a_w = U(m, (s + k, s + k + W)), vcol(O_CONVW + m * 31 + k)
                P.dve(lambda e, a_u=a_u, a_w=a_w: e.scalar_tensor_tensor(
                    out=a_y.ap, in0=a_u.ap, scalar=a_w.ap, in1=a_y.ap, op0=ALU.mult, op1=ALU.add),
                    reads=[a_u, a_w, a_y], writes=[a_y])

        for m in range(4):
            dg = DG[(m + 1) % 2]
            for k in range(31):
                a_d, a_w = dg(k), vcol(O_CONVW + m * 31 + k)
                P.pool(lambda e, a_d=a_d, a_w=a_w: e.tensor_scalar(
                    out=a_d.ap, in0=IDt[:], scalar1=a_w.ap, scalar2=0.0, op0=ALU.mult, op1=ALU.add),
                    reads=[IDN.whole(), a_w], writes=[a_d])
            if m < 3:
                for (s, W) in PT:
                    conv_tile(m, s, W)
        ffn_group(0, 0, True, False, load_part="lo")
        wos = [WCH.next() for _ in range(6)] + [
            View(f"s2k{i}", S2Kt[i][:].bitcast(BF16).rearrange("p (a b) -> p a b", a=8), 0, 2, (8, 128)) for i in (4, 5)]
        for d in range(NCH):
            load_w(wos[d].whole(), w_out[:, d * 128:(d + 1) * 128].rearrange("(kc p) n -> p kc n", p=128))
        lnt = S2K.items[:4]

        def ln_tile(s, W):
            pm, pq = PSG.next(), PSG.next()
            for m in range(4):
                ysq, ybf = S1K.next(), S1K.next()
                a_y, a_sq, a_bf = Y(m, (s, s + W)), ysq((0, W)), ybf((0, W))
                P.act(lambda e, a_y=a_y, a_sq=a_sq: e.activation(out=a_sq.ap, in_=a_y.ap, func=AF.Square),
                      reads=[a_y], writes=[a_sq])
                P.dve(lambda e, a_y=a_y, a_bf=a_bf: e.tensor_copy(out=a_bf.ap, in_=a_y.ap), reads=[a_y], writes=[a_bf])
                matmul(pm((0, W)), ONE.whole(), a_bf, m == 0, m == 3)
                matmul(pq((0, W)), ONE.whole(), a_sq, m == 0, m == 3)
            mean, var = lnt[0]((0, W)), lnt[1]((0, W))
            a_pm, a_pq = pm((0, W)), pq((0, W))
            P.dve(lambda e, a_pm=a_pm, mean=mean: e.tensor_scalar(
                out=mean.ap, in0=a_pm.ap, scalar1=1.0 / 512, scalar2=None, op0=ALU.mult), reads=[a_pm], writes=[mean])
            P.dve(lambda e, mean=mean, var=var: e.tensor_tensor(out=var.ap, in0=mean.ap, in1=mean.ap, op=ALU.mult),
                  reads=[mean], writes=[var])
            P.dve(lambda e, a_pq=a_pq, var=var: e.scalar_tensor_tensor(
                out=var.ap, in0=a_pq.ap, scalar=1.0 / 512, in1=var.ap, op0=ALU.mult, op1=ALU.subtract),
                reads=[a_pq, var], writes=[var])
            P.act(lambda e, var=var: e.activation(out=var.ap, in_=var.ap, func=AF.Ln, bias=1e-5),
                  reads=[var], writes=[var])
            P.act(lambda e, var=var: e.activation(out=var.ap, in_=var.ap, func=AF.Exp, scale=-0.5),
                  reads=[var], writes=[var])
            for m in range(4):
                t1_ = lnt[2 + m % 2]((0, W))
                a_y = Y(m, (s, s + W))
                P.dve(lambda e, a_y=a_y, mean=mean, t1_=t1_: e.tensor_tensor(
                    out=t1_.ap, in0=a_y.ap, in1=mean.ap, op=ALU.subtract), reads=[a_y, mean], writes=[t1_])
                P.dve(lambda e, var=var, t1_=t1_: e.tensor_tensor(
                    out=t1_.ap, in0=t1_.ap, in1=var.ap, op=ALU.mult), reads=[t1_, var], writes=[t1_])
                a_g, a_b, a_z = vcol(O_LNG + m), vcol(O_LNB + m), CATHI(m, (s, s + W))
                P.act(lambda e, t1_=t1_, a_g=a_g, a_b=a_b, a_z=a_z: e.activation(
                    out=a_z.ap, in_=t1_.ap, func=AF.Silu, scale=a_g.ap, bias=a_b.ap),
                    reads=[t1_, a_g, a_b], writes=[a_z])

        conv_tile(3, *NT[0])
        conv_tile(3, *NT[1])
        ln_tile(*NT[0])
        pending = []
        for ti, (s, W) in enumerate(NT):
            if ti + 2 < len(NT):
                conv_tile(3, *NT[ti + 2])
                if ti + 2 == len(NT) - 1:
                    ffn_group(0, 0, True, False, load_part="hi")
            if ti + 1 < len(NT):
                ln_tile(*NT[ti + 1])
            for d in range(NCH):
                pb = PSG.next()
                for kc in range(8):
                    rhs = CATLO(kc, (s, s + W)) if kc < 4 else CATHI(kc - 4, (s, s + W))
                    matmul(pb((0, W)), wos[d](kc), rhs, kc == 0, kc == 7)
                resid_add(d, s, W, pb((0, W)))
            pending.append((s, W))
            if ti + 1 >= len(NT) - 1:
                for (s2, W2) in pending:
                    norm_tile(O_FFN0, hn_dst, s2, W2, r_pool=S2K4)
                pending = []
        for (s2, W2) in pending:
            norm_tile(O_FFN0, hn_dst, s2, W2, r_pool=S2K4)

    GB_OFF = [0, 36864]

    NT = [(0, 416), (416, 412), (828, 412), (1240, 412), (1652, 412)]
    FT32 = RR([View(f"wch{i}", WCHt[i][:].rearrange("p a b -> p (a b)").bitcast(F32), 0, 4, (512,))
               for i in (0, 1, 5)])
    SQW = RR([View(f"wch{i}", WCHt[i][:].rearrange("p a b -> p (a b)")[:, h * 512:(h + 1) * 512], h * 1024, 2, (512,))
              for i in (4,) for h in (0, 1)])

    def ffn_group(l, gi, do_load, do_compute, norm_goff=None, after_down=None, load_part=None, pre_tile=None):
        if True:
            (c0, npair, bi) = GROUPS[gi]
            off = GB_OFF[bi]
            wup_ap = big(off, BF16, (8, 2, npair * 128))
            wdn_off = off + 8 * 2 * npair * 128 * 2
            wdn_ap = big(wdn_off, BF16, (npair, 1024))

            def wup_acc(kc, gv, p_):
                lo = off + (((kc * 2 + gv) * npair + p_) * 128) * 2
                return Acc(wup_ap[:, kc, gv, p_ * 128:(p_ + 1) * 128], "BIG", [(lo, lo + 256)])

            def wdn_acc(p_, d):
                lo = wdn_off + (p_ * 1024 + d * 128) * 2
                return Acc(wdn_ap[:, p_, d * 128:(d + 1) * 128], "BIG", [(lo, lo + 256)])

            if do_load:
                k0, k1 = {None: (0, 8), "lo": (0, 4), "hi": (4, 8)}[load_part]
                for gv in range(2):
                    col0 = gv * DFF + c0 * 128
                    dst = Acc(wup_ap[:, k0:k1, gv, :], "BIG",
                              [(off + ((kc * 2 + gv) * npair * 128) * 2, off + ((kc * 2 + gv + 1) * npair * 128) * 2)
                               for kc in range(k0, k1)])
                    load_w(dst, w_up[l, k0 * 128:k1 * 128, col0:col0 + npair * 128].rearrange("(kc p) n -> p kc n", p=128))
                if load_part != "lo":
                    dst = Acc(wdn_ap, "BIG", [(wdn_off, wdn_off + npair * 1024 * 2)])
                    load_w(dst, w_down[l, c0 * 128:(c0 + npair) * 128, :].rearrange("(j p) n -> p j n", p=128))
            if not do_compute:
                return

            def up_pair(ti, p_):
                ms, mw, ho = FT[ti]
                if True:
                    tiles = []
                    for gv in range(2):
                        pb = PS8.next()
                        for kc in range(8):
                            matmul(pb((0, mw)), wup_acc(kc, gv, p_), HN(kc, (ms, ms + mw)), kc == 0, kc == 7)
                        ch = gv * 22 + c0 + p_
                        w0, w1, w2 = (vcol(O_FCW + (l * 3 + k) * 44 + ch) for k in range(3))
                        bcol = vcol(O_FCB + l * 44 + ch)
                        T = S2K.next()
                        a_pb, a_t = pb((0, mw)), T((0, mw))
                        P.act(lambda e, a_pb=a_pb, a_t=a_t, w2=w2, bcol=bcol: e.activation(
                            out=a_t.ap, in_=a_pb.ap, func=AF.Identity, scale=w2.ap, bias=bcol.ap),
                            reads=[a_pb, w2, bcol], writes=[a_t])
                        for sh, wcol in ((1, w1), (2, w0)):
                            a_in, a_o = pb((0, mw - sh)), T((sh, mw))
                            P.dve(lambda e, a_in=a_in, a_o=a_o, wcol=wcol: e.scalar_tensor_tensor(
                                out=a_o.ap, in0=a_in.ap, scalar=wcol.ap, in1=a_o.ap, op0=ALU.mult, op1=ALU.add),
                                reads=[a_in, wcol, a_o], writes=[a_o])
                        tiles.append(T)
                    TG, TV = tiles
                    a_g, a_v = TG((ho, mw)), TV((ho, mw))
                    P.act(lambda e, a_g=a_g: e.activation(out=a_g.ap, in_=a_g.ap, func=AF.Silu),
                          reads=[a_g], writes=[a_g])
                    at = S1K.next()
                    a_a = at((0, mw - ho))
                    (P.pool if p_ % 2 == 1 else P.dve)(lambda e, a_g=a_g, a_v=a_v, a_a=a_a: e.tensor_tensor(
                        out=a_a.ap, in0=a_g.ap, in1=a_v.ap, op=ALU.mult), reads=[a_g, a_v], writes=[a_a])
                    return a_a

            def down_d(ti, acts, d):
                ms, mw, ho = FT[ti]
                os_, ow = ms + ho, mw - ho
                if True:
                    pb = PS8.next()
                    for p_ in range(npair):
                        matmul(pb((0, ow)), wdn_acc(p_, d), acts[p_], p_ == 0, p_ == npair - 1)
                    if d % 4 == 0:
                        resid_add(d, os_, ow, pb((0, ow)))
                        return
                    a_pb, a_t, a_h = pb((0, ow)), FT32.next()((0, ow)), H(d, (os_, os_ + ow))
                    P.act(lambda e, a_pb=a_pb, a_t=a_t: e.activation(out=a_t.ap, in_=a_pb.ap, func=AF.Identity),
                          reads=[a_pb], writes=[a_t])
                    P.pool(lambda e, a_t=a_t, a_h=a_h: e.tensor_tensor(out=a_h.ap, in0=a_t.ap, in1=a_h.ap, op=ALU.add),
                           reads=[a_t, a_h], writes=[a_h])

            if norm_goff is not None:
                norm_tile(norm_goff, hn_dst, *NT[0], sq_pool=SQW)
                norm_tile(norm_goff, hn_dst, *NT[1], sq_pool=SQW)
            if pre_tile is not None:
                pre_tile(0)
                pre_tile(1)
            prev = [up_pair(0, p_) for p_ in range(npair)]
            for ti in range(len(FT)):
                if norm_goff is not None and ti + 2 < len(NT):
                    norm_tile(norm_goff, hn_dst, *NT[ti + 2], sq_pool=SQW)
                if pre_tile is not None and ti + 2 < len(NT):
                    pre_tile(ti + 2)
                nxt = []
                dq = list(range(NCH))
                if ti + 1 < len(FT):
                    for p_ in range(npair):
                        nxt.append(up_pair(ti + 1, p_))
                        ndo = (NCH * (p_ + 1)) // npair - (NCH * p_) // npair if p_ >= 1 else 0
                        for _ in range(ndo):
                            if dq:
                                down_d(ti, prev, dq.pop(0))
                while dq:
                    down_d(ti, prev, dq.pop(0))
                if after_down is not None:
                    after_down(ti)
                prev = nxt

    def mixer1():
        XB = []
        for i in range(2):
            base = 38400 + i * 16640
            XB.append([View("BIG", big(base + k * 8320, F32, (L + 16,)), base + k * 8320, 4, (L + 16,)) for k in range(2)])
        RS = [S2K.items[ti] for ti in range(len(NT))]
        fin = None
        for ti, (s, W) in enumerate(NT):
            nf = norm_stats(s, W, RS[ti]((0, W)), defer=True, sq_split=True)
            if fin is not None:
                fin()
            fin = nf
        fin()
        for i in range(2):
            a_z = XB[i][0]((0, 16))
            P.dve(lambda e, a_z=a_z: e.memset(a_z.ap, 0.0), writes=[a_z])
        pws = []
        for g in range(4):
            ti_, hf = 2 + g // 2, g % 2
            pw_ap = WCHt[ti_][:].rearrange("p a b -> p (a b)")[:, hf * 512:(hf + 1) * 512].rearrange(
                "p (kc n) -> p kc n", kc=2)
            load_w(Acc(pw_ap, f"wch{ti_}", [(hf * 1024, hf * 1024 + 1024)]),
                   pool_w[g].rearrange("(kc p) n -> p kc n", p=128))
            pws.append((f"wch{ti_}", pw_ap, hf * 1024))

        def pw_acc(g, kc, dc):
            space, pw_ap, base = pws[g]
            lo = base + (kc * 256 + dc * 128) * 2
            return Acc(pw_ap[:, kc, dc * 128:(dc + 1) * 128], space, [(lo, lo + 256)])

        def window(g, kc, part):
            wwin = 2 ** (g + 1)
            if True:
                c = 2 * g + kc
                slot = (g % 2) * 2 + kc
                X, Bb = XB[c % 2]
                if part == "act":
                    a_s, a_m = Bb((32, 16 + L)), HN(slot, (16, L))
                    P.act(lambda e, a_s=a_s, a_m=a_m: e.activation(
                        out=a_m.ap, in_=a_s.ap, func=AF.Identity, scale=1.0 / wwin), reads=[a_s], writes=[a_m])
                    a_xs, a_nx = X((16, 16 + L)), HN(4 + slot, (0, L))
                    P.act(lambda e, a_xs=a_xs, a_nx=a_nx: e.activation(out=a_nx.ap, in_=a_xs.ap, func=AF.Identity, scale=-1.0),
                          reads=[a_xs], writes=[a_nx])
                    return
                for ti, (s_, W_) in enumerate(NT):
                    a_h, a_g, a_r, a_x = H(c, (s_, s_ + W_)), vcol(O_MNO + c), RS[ti]((0, W_)), X((16 + s_, 16 + s_ + W_))
                    P.dve(lambda e, a_h=a_h, a_g=a_g, a_r=a_r, a_x=a_x: e.scalar_tensor_tensor(
                        out=a_x.ap, in0=a_h.ap, scalar=a_g.ap, in1=a_r.ap, op0=ALU.mult, op1=ALU.mult),
                        reads=[a_h, a_g, a_r], writes=[a_x])
                a_a, a_b, a_o = X((16, 16 + L)), X((16 - wwin, 16 - wwin + L)), Bb((16, 16 + L))
                if g == 0:
                    a_b1 = X((15, 15 + L))
                    P.dve(lambda e, a_a=a_a, a_b1=a_b1, a_o=a_o: e.tensor_tensor(
                        out=a_o.ap, in0=a_a.ap, in1=a_b1.ap, op=ALU.add), reads=[a_a, a_b1], writes=[a_o])
                else:
                    P.dve(lambda e, a_a=a_a, a_b=a_b, a_o=a_o: e.tensor_tensor_scan(
                        out=a_o.ap, data0=a_a.ap, data1=a_b.ap, initial=0.0, op0=ALU.add, op1=ALU.subtract),
                        reads=[a_a, a_b], writes=[a_o])
                a_s16, a_ic, a_m16 = Bb((16, 32)), VEC((O_INVC + g * 16, O_INVC + g * 16 + 16)), HN(slot, (0, 16))
                P.dve(lambda e, a_s16=a_s16, a_ic=a_ic, a_m16=a_m16: e.tensor_tensor(
                    out=a_m16.ap, in0=a_s16.ap, in1=a_ic.ap, op=ALU.mult), reads=[a_s16, a_ic], writes=[a_m16])

        def project(g, only_tile=None):
            for ti, (s, W) in enumerate(NT):
                if only_tile is not None and ti != only_tile:
                    continue
                for dc in range(2):
                    ch = 2 * g + dc
                    pb = PSG.next()
                    for kc in range(2):
                        matmul(pb((0, W)), pw_acc(g, kc, dc), HN((g % 2) * 2 + kc, (s, s + W)), kc == 0, False)
                    for kc in range(2):
                        matmul(pb((0, W)), pw_acc(g, kc, dc), HN(4 + (g % 2) * 2 + kc, (s, s + W)), False, kc == 1)
                    t = FT32.next()((0, W))
                    a_pb, a_sc, a_bs = pb((0, W)), vcol(O_POOLS + ch), vcol(O_PBS + ch)
                    P.act(lambda e, a_pb=a_pb, t=t, a_sc=a_sc, a_bs=a_bs: e.activation(
                        out=t.ap, in_=a_pb.ap, func=AF.Identity, scale=a_sc.ap, bias=a_bs.ap),
                        reads=[a_pb, a_sc, a_bs], writes=[t])
                    a_h = H(ch, (s, s + W))
                    P.pool(lambda e, t=t, a_h=a_h: e.tensor_tensor(out=a_h.ap, in0=t.ap, in1=a_h.ap, op=ALU.add),
                           reads=[t, a_h], writes=[a_h])

        for kc in range(2):
            window(0, kc, "dve")
            window(0, kc, "act")
        for g in range(3):
            window(g + 1, 0, "dve")
            project(g)
            window(g + 1, 0, "act")
            window(g + 1, 1, "dve")
            window(g + 1, 1, "act")
        project(3)

        def tail_tile(ti):
            s, W = NT[ti]
            norm_tile(O_FFN1, hn_dst, s, W, sq_pool=SQW)
        mixer1_tail.append(tail_tile)

    final_outs = []

    def final_tile(ti):
        s, W = NT[ti]
        norm_tile(O_FIN, lambda c, s_, W_: H(c, (s_, s_ + W_)), s, W, sq_pool=SQW)
        a = max(s, NMETA)
        final_outs.append(P.dma("sp", outT[:, a - NMETA:s + W - NMETA].rearrange("(c p) t -> p c t", p=128),
                                Ht[:, :, a:s + W], reads=[H((0, NCH), (a, s + W))]))

    def final():
        P.fence("sp", final_outs)

    def ffn0():
        ffn_group(0, 1, True, False)
        ffn_group(0, 0, False, True)
        ffn_group(0, 2, True, False)
        ffn_group(0, 1, False, True)
        ffn_group(0, 3, True, False)
        ffn_group(0, 2, False, True)
        ffn_group(1, 0, True, False)
        ffn_group(0, 3, False, True)

    mixer1_tail = []

    def mixer1_and_prefetch():
        mixer1()
        ffn_group(1, 1, True, False)
        if stage == "mix1":
            for ti in range(len(NT)):
                mixer1_tail[0](ti)

    def ffn1():
        ffn_group(1, 0, False, True, pre_tile=mixer1_tail[0])
        ffn_group(1, 2, True, False)
        ffn_group(1, 1, False, True)
        ffn_group(1, 3, True, False)
        ffn_group(1, 2, False, True)
        ffn_group(1, 3, False, True, after_down=final_tile)

    stages = [("mix0", mixer0), ("ffn0", ffn0), ("mix1", mixer1_and_prefetch), ("ffn1", ffn1)]
    done = False
    for name, fn in stages:
        fn()
        if stage == name:
            dump_and_finish()
            done = True
            break
    if not done:
        final()
        if stage is not None:
            dump_and_finish()

    sems = {e: es.enter_context(nc.semaphore(f"s_{e}")) for e in ("pe", "act", "dve", "pool")}
    dma_sems = {q: [es.enter_context(nc.semaphore(f"d_{q}{i}")) for i in range(NDMASEM)] for q in ("sp", "pool")}
    P.finalize(sems, dma_sems)
    with nc.Block() as block:
        @block.sync
        def _(e):
            P.emit("sp", e)

        @block.gpsimd
        def _(e):
            P.emit("pool", e)

        @block.tensor
        def _(e):
            P.emit("pe", e)

        @block.scalar
        def _(e):
            P.emit("act", e)

        @block.vector
        def _(e):
            P.emit("dve", e)
    es.close()
    return nc


def _cols(v):
    v = np.asarray(v, np.float32).reshape(-1, 128)
    return np.ascontiguousarray(v.T)


def pack_vecs(inp):
    V = np.zeros((128, NVH), np.float32)
    V[:, O_MNE:O_MNE + 8] = _cols(inp["mix_norm_even"][0])
    V[:, O_MNO:O_MNO + 8] = _cols(inp["mix_norm_odd"][0])
    V[:, O_FFN0:O_FFN0 + 8] = _cols(inp["ffn_norm"][0])
    V[:, O_FFN1:O_FFN1 + 8] = _cols(inp["ffn_norm"][1])
    V[:, O_FIN:O_FIN + 8] = _cols(inp["final_norm"])
    cw = np.asarray(inp["conv_w"][0], np.float32)
    V[:, O_CONVW:O_CONVW + 124] = cw.reshape(31, 4, 128).transpose(2, 1, 0).reshape(128, 124)
    V[:, O_CONVB:O_CONVB + 4] = _cols(inp["conv_b"][0])
    V[:, O_LNG:O_LNG + 4] = _cols(inp["ln_g"][0])
    V[:, O_LNB:O_LNB + 4] = _cols(inp["ln_b"][0])
    V[:, O_POOLB:O_POOLB + 8] = _cols(np.asarray(inp["pool_b"][0]).reshape(-1))
    V[:, O_POOLS:O_POOLS + 8] = _cols(inp["pool_scale"][0])
    fw = np.asarray(inp["ffn_conv_w"], np.float32)
    V[:, O_FCW:O_FCW + 264] = fw.reshape(2, 3, 44, 128).transpose(3, 0, 1, 2).reshape(128, 264)
    fb = np.asarray(inp["ffn_conv_b"], np.float32)
    V[:, O_FCB:O_FCB + 88] = fb.reshape(2, 44, 128).transpose(2, 0, 1).reshape(128, 88)
    V[0:8, O_BF] = np.asarray(inp["b_f"][0], np.float32)
    for g, w in enumerate((2, 4, 8, 16)):
        V[:, O_INVC + g * 16:O_INVC + (g + 1) * 16] = (1.0 / np.minimum(np.arange(1, 17), w)).astype(np.float32)[None, :]
    return V


def make_in_maps(inp, cores):
    x = np.asarray(inp["x"], np.float32)
    vecs = pack_vecs(inp)
    shared = {
        "metaT": np.ascontiguousarray(np.asarray(inp["meta_tokens"], np.float32).T),
        "vecs": vecs,
        "w_in": np.ascontiguousarray(np.asarray(inp["w_in"][0], np.float32)),
        "w_out": np.ascontiguousarray(np.asarray(inp["w_out"][0], np.float32)),
        "pool_w": np.ascontiguousarray(np.asarray(inp["pool_w"][0], np.float32)),
        "w_up": np.ascontiguousarray(np.asarray(inp["w_up"], np.float32)),
        "w_down": np.ascontiguousarray(np.asarray(inp["w_down"], np.float32)),
    }
    maps = []
    for b in cores:
        m = dict(shared)
        m["xT"] = np.ascontiguousarray(x[b].T)
        maps.append(m)
    return maps


_NC_CACHE = {}


def kernel(**inputs):
    if "full" not in _NC_CACHE:
        _NC_CACHE["full"] = build_program(None)
    nc = _NC_CACHE["full"]
    n = 8
    in_maps = make_in_maps(inputs, list(range(n)))
    res = run_bass_kernel_spmd(nc, in_maps, core_ids=list(range(n)))
    out = np.empty((n, SEQ, D), np.float32)
    for b in range(n):
        out[b] = res.results[b]["outT"].T
    return out
```
